# Optimizing a Trainium2 kernel written in Bass

```python
import math
import jax
import jax.numpy as jnp
from jax import lax
import numpy as np

D_MODEL = 1024
BATCH = 4
SEQ = 8192
DEPTH = 4

GRID_W = 64
CTX_LEN = 256
N_EVEN = (DEPTH + 1) // 2
N_ODD = DEPTH // 2
N_MOD = 9
NORM_EPS = 1e-6
MACARON = 0.5
FFN_HIDDEN = 256 * ((8 * D_MODEL // 3 + 255) // 256)

DA_HEADS = D_MODEL // 128
DA_HEAD_DIM = 64
DA_WIDTH = DA_HEADS * 2 * DA_HEAD_DIM
DA_SUBLN_EPS = 1e-5
Q_BLOCK = 128
ROPE_BASE = 10000.0

SSD_HEAD_DIM = 64
SSD_INNER = D_MODEL
SSD_HEADS = SSD_INNER // SSD_HEAD_DIM
SSD_GROUPS = 2
SSD_STATE = 128
SSD_CONV = 5
SSD_CHUNK = 128
SSD_CONV_DIM = SSD_INNER + 2 * SSD_GROUPS * SSD_STATE
SSD_NORM_EPS = 1e-5
SSD_DT_MIN = 0.001
SSD_DT_MAX = 0.1

EVEN_IN = 3 * DA_WIDTH + SSD_INNER + SSD_CONV_DIM + 2 * SSD_HEADS
EVEN_OUT = DA_WIDTH + SSD_INNER

TM_HEAD = 64
TM_HEADS = D_MODEL // TM_HEAD
TM_DECAY_LORA = 64
TM_AAA_LORA = 64
TM_MV_LORA = 32
TM_GATE_LORA = 160
TM_GN_EPS = 64e-5

kernel_name = 'hybrid_diffattn_ssd_rwkv7_macaron_dit'


def rms_norm(x, w, eps=NORM_EPS):
    xf = x.astype(jnp.float32)
    y = xf * lax.rsqrt(jnp.mean(xf * xf, axis=-1, keepdims=True) + eps)
    return (y * w.astype(jnp.float32)).astype(x.dtype)


def pre_norm(x, mod, nw, s):
    h = rms_norm(x, nw[2 * s])
    return (h * (1 + mod[:, :, 3 * s + 1]) + mod[:, :, 3 * s]).astype(x.dtype)


def post_add(x, y, mod, nw, s, weight):
    return (x + weight * mod[:, :, 3 * s + 2] * rms_norm(y, nw[2 * s + 1])).astype(x.dtype)


def swiglu(h, w_in, w_out):
    gate, up = jnp.split(h @ w_in, 2, axis=-1)
    return (jax.nn.silu(gate) * up) @ w_out


def ffn_sublayer(x, mod, nw, s, w_in, w_out):
    return post_add(x, swiglu(pre_norm(x, mod, nw, s), w_in, w_out), mod, nw, s, MACARON)


def axial_rope_angles(length, dh):
    rows = length // GRID_W
    row = jnp.repeat(jnp.arange(rows, dtype=jnp.float32), GRID_W)
    col = jnp.tile(jnp.arange(GRID_W, dtype=jnp.float32), rows)
    n_freq = dh // 4
    inv_freq = ROPE_BASE ** (-jnp.arange(n_freq, dtype=jnp.float32) / n_freq)
    return row[:, None] * inv_freq, col[:, None] * inv_freq


def rope_rotate(x, ang):
    x1, x2 = jnp.split(x, 2, axis=-1)
    cos = jnp.cos(ang)[:, None, None, :].astype(x.dtype)
    sin = jnp.sin(ang)[:, None, None, :].astype(x.dtype)
    return jnp.concatenate([x1 * cos - x2 * sin, x2 * cos + x1 * sin], axis=-1)


def axial_rope(x, ang_row, ang_col):
    half = x.shape[-1] // 2
    return jnp.concatenate([rope_rotate(x[..., :half], ang_row), rope_rotate(x[..., half:], ang_col)], axis=-1)


def diff_attention(q_lat, k_lat, v_lat, q_ctx, k_ctx, v_ctx, lam, lam_init, subln_w):
    scale = DA_HEAD_DIM ** -0.5

    def attend(q, k, v):
        s = jnp.einsum('bqhmd,bkhmd->bhmqk', q, k).astype(jnp.float32) * scale
        p = jax.nn.softmax(s, axis=-1)
        a = (p[:, :, 0] - lam * p[:, :, 1]).astype(v.dtype)
        return jnp.einsum('bhqk,bkhe->bqhe', a, v)

    bsz, seq_len = q_lat.shape[:2]
    n_blocks = seq_len // Q_BLOCK
    k_all = jnp.concatenate([k_ctx, k_lat], axis=1)
    v_all = jnp.concatenate([v_ctx, v_lat], axis=1)
    q_blocks = jnp.moveaxis(q_lat.reshape(bsz, n_blocks, Q_BLOCK, DA_HEADS, 2, DA_HEAD_DIM), 1, 0)
    o_lat = lax.map(lambda qb: attend(qb, k_all, v_all), q_blocks)
    o_lat = jnp.moveaxis(o_lat, 0, 1).reshape(bsz, seq_len, DA_HEADS, 2 * DA_HEAD_DIM)
    o_ctx = attend(q_ctx, k_ctx, v_ctx)

    def finish(o):
        o = rms_norm(o, subln_w, DA_SUBLN_EPS) * (1.0 - lam_init)
        return o.reshape(o.shape[0], o.shape[1], DA_WIDTH)
    return finish(o_lat), finish(o_ctx)


def depthwise_conv(x, w, b):
    pad = SSD_CONV // 2
    y = lax.conv_general_dilated(x, w[:, None, :].astype(x.dtype), window_strides=(1,), padding=[(pad, pad)],
                                 dimension_numbers=('NWC', 'WIO', 'NWC'), feature_group_count=x.shape[-1])
    return y + b.astype(x.dtype)


def ssd_prepare(xbc, dt_raw, conv_w, conv_b, dt_bias):
    bsz, length, _ = xbc.shape
    xbc = jax.nn.silu(depthwise_conv(xbc, conv_w, conv_b))
    xs, bm, cm = jnp.split(xbc, [SSD_INNER, SSD_INNER + SSD_GROUPS * SSD_STATE], axis=-1)
    xs = xs.reshape(bsz, length, SSD_HEADS, SSD_HEAD_DIM)
    bm = bm.reshape(bsz, length, SSD_GROUPS, SSD_STATE)
    cm = cm.reshape(bsz, length, SSD_GROUPS, SSD_STATE)
    dt = jax.nn.softplus(dt_raw.astype(jnp.float32).reshape(bsz, length, 2, SSD_HEADS) + dt_bias.astype(jnp.float32))
    return xs, bm, cm, dt


def ssd_scan(x, dt, a, bm, cm, h0):
    f32 = jnp.float32
    bsz, length = x.shape[:2]
    nc = length // SSD_CHUNK
    r = SSD_HEADS // SSD_GROUPS
    xc = x.astype(f32).reshape(bsz, nc, SSD_CHUNK, SSD_GROUPS, r, SSD_HEAD_DIM)
    dtc = dt.astype(f32).reshape(bsz, nc, SSD_CHUNK, SSD_GROUPS, r)
    bc = bm.astype(f32).reshape(bsz, nc, SSD_CHUNK, SSD_GROUPS, SSD_STATE)
    cc = cm.astype(f32).reshape(bsz, nc, SSD_CHUNK, SSD_GROUPS, SSD_STATE)
    acs = jnp.cumsum(dtc * a.reshape(SSD_GROUPS, r), axis=2)
    lower = jnp.tril(jnp.ones((SSD_CHUNK, SSD_CHUNK), dtype=bool))
    seg = acs[:, :, :, None] - acs[:, :, None, :]
    decay = jnp.exp(jnp.where(lower[:, :, None, None], seg, -jnp.inf))
    cb = jnp.einsum('bcign,bcjgn->bcijg', cc, bc)
    y_diag = jnp.einsum('bcijgr,bcjgrp->bcigrp', cb[..., None] * decay * dtc[:, :, None], xc)
    to_end = jnp.exp(acs[:, :, -1:] - acs) * dtc
    states = jnp.einsum('bcjgn,bcjgrp->bcgrpn', bc, xc * to_end[..., None])
    chunk_decay = jnp.exp(acs[:, :, -1])

    def step(h, inp):
        s, d = inp
        return h * d[..., None, None] + s, h

    h_last, h_in = lax.scan(step, h0.astype(f32), (jnp.moveaxis(states, 1, 0), jnp.moveaxis(chunk_decay, 1, 0)))
    h_in = jnp.moveaxis(h_in, 0, 1)
    y_off = jnp.einsum('bcign,bcgrpn->bcigrp', cc, h_in) * jnp.exp(acs)[..., None]
    return (y_diag + y_off).reshape(bsz, length, SSD_HEADS, SSD_HEAD_DIM), h_last


def ssd_two_way(xs, bm, cm, dt, a, h0_fwd, h0_bwd):
    rev = lambda t: jnp.flip(t, axis=1)
    y_f, h_f = ssd_scan(xs, dt[:, :, 0], a[0], bm, cm, h0_fwd)
    y_b, h_b = ssd_scan(rev(xs), rev(dt[:, :, 1]), a[1], rev(bm), rev(cm), h0_bwd)
    return y_f + rev(y_b), h_f, h_b


def ssd_output(y, xs, z, d_skip, norm_w):
    bsz, length, _ = z.shape
    y = y + d_skip.astype(jnp.float32)[:, None] * xs.astype(jnp.float32)
    y = y.reshape(bsz, length, SSD_INNER) * jax.nn.silu(z.astype(jnp.float32))
    g = y.reshape(bsz, length, SSD_GROUPS, SSD_INNER // SSD_GROUPS)
    g = g * lax.rsqrt(jnp.mean(g * g, axis=-1, keepdims=True) + SSD_NORM_EPS)
    return (g.reshape(bsz, length, SSD_INNER) * norm_w.astype(jnp.float32)).astype(z.dtype)


def diff_ssd_mixer(h_lat, h_ctx, ang_row, ang_col, layer, w_in, w_out, lam_p, subln_w,
                   conv_w, conv_b, dt_bias, a_log, d_skip, norm_w):
    splits = [DA_WIDTH, 2 * DA_WIDTH, 3 * DA_WIDTH, 3 * DA_WIDTH + SSD_INNER,
              3 * DA_WIDTH + SSD_INNER + SSD_CONV_DIM]

    def project(h):
        bsz, length, _ = h.shape
        q, k, v, z, xbc, dt_raw = jnp.split(h @ w_in, splits, axis=-1)
        q = q.reshape(bsz, length, DA_HEADS, 2, DA_HEAD_DIM)
        k = k.reshape(bsz, length, DA_HEADS, 2, DA_HEAD_DIM)
        v = v.reshape(bsz, length, DA_HEADS, 2 * DA_HEAD_DIM)
        return q, k, v, z, xbc, dt_raw

    q_l, k_l, v_l, z_l, xbc_l, dt_l = project(h_lat)
    q_c, k_c, v_c, z_c, xbc_c, dt_c = project(h_ctx)
    q_l = axial_rope(q_l, ang_row, ang_col)
    k_l = axial_rope(k_l, ang_row, ang_col)

    lam_init = 0.8 - 0.6 * math.exp(-0.3 * layer)
    lp = lam_p.astype(jnp.float32)
    lam = jnp.exp(jnp.sum(lp[0] * lp[1])) - jnp.exp(jnp.sum(lp[2] * lp[3])) + lam_init
    att_l, att_c = diff_attention(q_l, k_l, v_l, q_c, k_c, v_c, lam, lam_init, subln_w)

    a = -jnp.exp(a_log.astype(jnp.float32))
    xs_c, b_c, c_c, dtp_c = ssd_prepare(xbc_c, dt_c, conv_w, conv_b, dt_bias)
    xs_l, b_l, c_l, dtp_l = ssd_prepare(xbc_l, dt_l, conv_w, conv_b, dt_bias)
    h0 = jnp.zeros((h_lat.shape[0], SSD_GROUPS, SSD_HEADS // SSD_GROUPS, SSD_HEAD_DIM, SSD_STATE), jnp.float32)
    y_c, hf_c, hb_c = ssd_two_way(xs_c, b_c, c_c, dtp_c, a, h0, h0)
    y_l, _, _ = ssd_two_way(xs_l, b_l, c_l, dtp_l, a, hf_c, hb_c)
    ssd_l = ssd_output(y_l, xs_l, z_l, d_skip, norm_w)
    ssd_c = ssd_output(y_c, xs_c, z_c, d_skip, norm_w)
    out_l = jnp.concatenate([att_l, ssd_l], axis=-1) @ w_out
    out_c = jnp.concatenate([att_c, ssd_c], axis=-1) @ w_out
    return out_l, out_c


def token_shift_bidir(x):
    xp = jnp.pad(x, ((0, 0), (1, 1), (0, 0)))
    return 0.5 * (xp[:, :-2] + xp[:, 2:])


def split_heads(t):
    return t.reshape(*t.shape[:-1], TM_HEADS, TM_HEAD)


def rwkv_prepare(x, tm, v_first, v_params):
    f32 = jnp.float32
    xx = token_shift_bidir(x) - x
    mu = tm['mu']
    xr, xw, xk, xv, xa, xg = [x + xx * mu[i] for i in range(6)]
    r = xr @ tm['w_r']
    k = xk @ tm['w_k']
    v_raw = xv @ tm['w_v']
    v = v_raw
    if v_params is not None:
        v0, v1, v2 = v_params
        v = v_raw + (v_first - v_raw) * jax.nn.sigmoid(v0 + (xv @ v1) @ v2)
    w_lora = jnp.einsum('eblr,erd->ebld', jnp.tanh(jnp.einsum('bld,edr->eblr', xw, tm['w1'])), tm['w2'])
    w_log = -jax.nn.softplus(-(tm['w0'][:, None, None, :] + w_lora).astype(f32)) - 0.5
    decay = jnp.exp(-jnp.exp(w_log))
    a_lora = jnp.einsum('eblr,erd->ebld', jnp.einsum('bld,edr->eblr', xa, tm['a1']), tm['a2'])
    a = jax.nn.sigmoid((tm['a0'][:, None, None, :] + a_lora).astype(f32))
    g = jax.nn.sigmoid(xg @ tm['g1']) @ tm['g2']
    kk = split_heads((k * tm['k_k']).astype(f32))
    kk = (kk / jnp.maximum(jnp.sqrt(jnp.sum(kk * kk, axis=-1, keepdims=True)), 1e-12)).reshape(k.shape)
    k_dir = k.astype(f32)[None] * (1.0 + (a - 1.0) * tm['k_a'].astype(f32))
    return r, decay, k_dir, v, kk, a, g, v_raw


def wkv_bidir(r, w, k, v, kk, a, s0):
    f32 = jnp.float32
    bsz, length, _ = r.shape

    def both(t):
        t = t.astype(f32)
        return jnp.stack([t, jnp.flip(t, axis=1)])

    def each(t):
        t = t.astype(f32)
        return jnp.stack([t[0], jnp.flip(t[1], axis=1)])

    def steps(t):
        return jnp.moveaxis(t.reshape(2, bsz, length, TM_HEADS, TM_HEAD), 2, 0)

    xs = tuple(steps(t) for t in (both(r), each(w), each(k), both(v), both(kk), each(a)))

    def step(s, inp):
        r_t, w_t, k_t, v_t, kk_t, a_t = inp
        sa = jnp.einsum('ebhvk,ebhk->ebhv', s, -kk_t)
        s = s * w_t[..., None, :] + sa[..., None] * (kk_t * a_t)[..., None, :] + v_t[..., None] * k_t[..., None, :]
        return s, jnp.einsum('ebhvk,ebhk->ebhv', s, r_t)

    s_fin, ys = lax.scan(step, s0, xs)
    y = ys[:, 0] + jnp.flip(ys[:, 1], axis=0)
    return jnp.moveaxis(y, 0, 1).reshape(bsz, length, D_MODEL), s_fin


def rwkv_output(y, r, k_dir, v, g, tm):
    f32 = jnp.float32
    bsz, length, _ = y.shape
    yh = split_heads(y)
    mean = jnp.mean(yh, axis=-1, keepdims=True)
    var = jnp.mean(jnp.square(yh - mean), axis=-1, keepdims=True)
    yn = ((yh - mean) * lax.rsqrt(var + TM_GN_EPS)).reshape(bsz, length, D_MODEL)
    yn = yn * tm['ln_w'].astype(f32) + tm['ln_b'].astype(f32)
    rk = jnp.einsum('blhn,eblhn->blh', split_heads(r.astype(f32) * tm['r_k'].astype(f32)), split_heads(k_dir))
    bonus = (rk[..., None] * split_heads(v.astype(f32))).reshape(bsz, length, D_MODEL)
    return ((yn + bonus) * g.astype(f32)).astype(r.dtype)


def rwkv_mixer(h_lat, h_ctx, tm, v_first, v_params, need_ctx):
    vf_l, vf_c = v_first
    r_c, w_c, k_c, v_c, kk_c, a_c, g_c, vraw_c = rwkv_prepare(h_ctx, tm, vf_c, v_params)
    r_l, w_l, k_l, v_l, kk_l, a_l, g_l, vraw_l = rwkv_prepare(h_lat, tm, vf_l, v_params)
    s0 = jnp.zeros((2, h_lat.shape[0], TM_HEADS, TM_HEAD, TM_HEAD), jnp.float32)
    y_c, s_c = wkv_bidir(r_c, w_c, k_c, v_c, kk_c, a_c, s0)
    y_l, _ = wkv_bidir(r_l, w_l, k_l, v_l, kk_l, a_l, s_c)
    out_l = rwkv_output(y_l, r_l, k_l, v_l, g_l, tm) @ tm['w_o']
    out_c = rwkv_output(y_c, r_c, k_c, v_c, g_c, tm) @ tm['w_o'] if need_ctx else None
    return out_l, out_c, vraw_l, vraw_c


def setup_inputs(seed: int = 0) -> dict:
    key = jax.random.key(seed)
    keys = iter(jax.random.split(key, 64))
    f32 = jnp.float32
    D, F = D_MODEL, FFN_HIDDEN

    def normal(shape, scale):
        return scale * jax.random.normal(next(keys), shape, f32)

    def gain(shape):
        return 1.0 + normal(shape, 0.02)

    dt = jnp.exp(jax.random.uniform(next(keys), (N_EVEN, 2, SSD_HEADS), f32, math.log(SSD_DT_MIN), math.log(SSD_DT_MAX)))
    ssd_dt_bias = dt + jnp.log(-jnp.expm1(-dt))
    ssd_a_log = jnp.log(jax.random.uniform(next(keys), (N_EVEN, 2, SSD_HEADS), f32, 1.0, 16.0))
    tm_mu = jax.random.uniform(next(keys), (N_ODD, 6, D), f32)
    tm_w0 = jnp.linspace(-6.5, -1.5, D, dtype=f32) + normal((N_ODD, 2, D), 0.1)
    return {
        'x': normal((BATCH, SEQ, D), 1.0),
        'c': normal((BATCH, D), 1.0),
        'ctx': normal((BATCH, CTX_LEN, D), 1.0),
        'c_ctx': normal((D,), 1.0),
        'ada_w': normal((DEPTH, D, N_MOD * D), 0.5 * D ** -0.5),
        'ada_b': normal((DEPTH, N_MOD * D), 0.02),
        'norm_w': gain((DEPTH, 6, D)),
        'ffn_w_in': normal((DEPTH, 2, D, 2 * F), D ** -0.5),
        'ffn_w_out': normal((DEPTH, 2, F, D), F ** -0.5),
        'ev_w_in': normal((N_EVEN, D, EVEN_IN), D ** -0.5),
        'ev_w_out': normal((N_EVEN, EVEN_OUT, D), EVEN_OUT ** -0.5),
        'da_lambda': normal((N_EVEN, 4, DA_HEAD_DIM), 0.1),
        'da_subln': gain((N_EVEN, 2 * DA_HEAD_DIM)),
        'ssd_conv_w': normal((N_EVEN, SSD_CONV, SSD_CONV_DIM), SSD_CONV ** -0.5),
        'ssd_conv_b': normal((N_EVEN, SSD_CONV_DIM), 0.02),
        'ssd_dt_bias': ssd_dt_bias,
        'ssd_a_log': ssd_a_log,
        'ssd_d': 1.0 + normal((N_EVEN, SSD_HEADS), 0.1),
        'ssd_norm': gain((N_EVEN, SSD_INNER)),
        'tm_mu': tm_mu,
        'tm_w_r': normal((N_ODD, D, D), D ** -0.5),
        'tm_w_k': normal((N_ODD, D, D), D ** -0.5),
        'tm_w_v': normal((N_ODD, D, D), D ** -0.5),
        'tm_w_o': normal((N_ODD, D, D), D ** -0.5),
        'tm_w0': tm_w0,
        'tm_w1': normal((N_ODD, 2, D, TM_DECAY_LORA), D ** -0.5),
        'tm_w2': normal((N_ODD, 2, TM_DECAY_LORA, D), 0.1 * TM_DECAY_LORA ** -0.5),
        'tm_a0': normal((N_ODD, 2, D), 0.1),
        'tm_a1': normal((N_ODD, 2, D, TM_AAA_LORA), D ** -0.5),
        'tm_a2': normal((N_ODD, 2, TM_AAA_LORA, D), 0.1 * TM_AAA_LORA ** -0.5),
        'tm_g1': normal((N_ODD, D, TM_GATE_LORA), D ** -0.5),
        'tm_g2': normal((N_ODD, TM_GATE_LORA, D), TM_GATE_LORA ** -0.5),
        'tm_k_k': 0.85 + normal((N_ODD, D), 0.02),
        'tm_k_a': 1.0 + normal((N_ODD, D), 0.02),
        'tm_r_k': normal((N_ODD, D), 0.1),
        'tm_ln_w': gain((N_ODD, D)),
        'tm_ln_b': normal((N_ODD, D), 0.02),
        'tm_v0': 1.0 + normal((N_ODD - 1, D), 0.1),
        'tm_v1': normal((N_ODD - 1, D, TM_MV_LORA), D ** -0.5),
        'tm_v2': normal((N_ODD - 1, TM_MV_LORA, D), 0.1 * TM_MV_LORA ** -0.5),
    }


def reference(x, c, ctx, c_ctx, ada_w, ada_b, norm_w, ffn_w_in, ffn_w_out,
              ev_w_in, ev_w_out, da_lambda, da_subln, ssd_conv_w, ssd_conv_b,
              ssd_dt_bias, ssd_a_log, ssd_d, ssd_norm,
              tm_mu, tm_w_r, tm_w_k, tm_w_v, tm_w_o, tm_w0, tm_w1, tm_w2,
              tm_a0, tm_a1, tm_a2, tm_g1, tm_g2, tm_k_k, tm_k_a, tm_r_k,
              tm_ln_w, tm_ln_b, tm_v0, tm_v1, tm_v2):
    bsz, seq_len, _ = x.shape
    ang_row, ang_col = axial_rope_angles(seq_len, DA_HEAD_DIM)
    x_lat, x_ctx = x, ctx
    v_first = (None, None)
    for l in range(DEPTH):
        last = l == DEPTH - 1
        j = l // 2
        mod_l = (jax.nn.silu(c) @ ada_w[l] + ada_b[l]).reshape(bsz, 1, N_MOD, D_MODEL)
        mod_c = (jax.nn.silu(c_ctx) @ ada_w[l] + ada_b[l]).reshape(1, 1, N_MOD, D_MODEL)
        nw = norm_w[l]
        x_lat = ffn_sublayer(x_lat, mod_l, nw, 0, ffn_w_in[l, 0], ffn_w_out[l, 0])
        x_ctx = ffn_sublayer(x_ctx, mod_c, nw, 0, ffn_w_in[l, 0], ffn_w_out[l, 0])
        h_lat = pre_norm(x_lat, mod_l, nw, 1)
        h_ctx = pre_norm(x_ctx, mod_c, nw, 1)
        if l % 2 == 0:
            y_lat, y_ctx = diff_ssd_mixer(h_lat, h_ctx, ang_row, ang_col, l, ev_w_in[j], ev_w_out[j],
                                          da_lambda[j], da_subln[j], ssd_conv_w[j], ssd_conv_b[j],
                                          ssd_dt_bias[j], ssd_a_log[j], ssd_d[j], ssd_norm[j])
        else:
            tm = {'mu': tm_mu[j], 'w_r': tm_w_r[j], 'w_k': tm_w_k[j], 'w_v': tm_w_v[j], 'w_o': tm_w_o[j],
                  'w0': tm_w0[j], 'w1': tm_w1[j], 'w2': tm_w2[j], 'a0': tm_a0[j], 'a1': tm_a1[j], 'a2': tm_a2[j],
                  'g1': tm_g1[j], 'g2': tm_g2[j], 'k_k': tm_k_k[j], 'k_a': tm_k_a[j], 'r_k': tm_r_k[j],
                  'ln_w': tm_ln_w[j], 'ln_b': tm_ln_b[j]}
            v_params = (tm_v0[j - 1], tm_v1[j - 1], tm_v2[j - 1]) if j > 0 else None
            y_lat, y_ctx, v_l, v_c = rwkv_mixer(h_lat, h_ctx, tm, v_first, v_params, not last)
            if j == 0:
                v_first = (v_l, v_c)
        x_lat = post_add(x_lat, y_lat, mod_l, nw, 1, 1.0)
        x_lat = ffn_sublayer(x_lat, mod_l, nw, 2, ffn_w_in[l, 1], ffn_w_out[l, 1])
        if not last:
            x_ctx = post_add(x_ctx, y_ctx, mod_c, nw, 1, 1.0)
            x_ctx = ffn_sublayer(x_ctx, mod_c, nw, 2, ffn_w_in[l, 1], ffn_w_out[l, 1])
    return x_lat
```

```python
import math
import numpy as np
from contextlib import ExitStack
import concourse.bass as bass
import concourse.mybir as mybir
from concourse.bass_utils import run_bass_kernel_spmd

F32 = mybir.dt.float32
F32R = mybir.dt.float32r
AF = mybir.ActivationFunctionType
ALU = mybir.AluOpType
AX = mybir.AxisListType
NDS = 24

D = 1024
FH = 2816
NMOD = 9
CTX = 256
NCORES = 8


class Buf:
    __slots__ = ("w", "r")

    def __init__(self):
        self.w = None
        self.r = {}


class View:
    __slots__ = ("ap", "bufs")

    def __init__(self, ap, bufs):
        self.ap = ap
        self.bufs = bufs

    def __getitem__(self, k):
        return View(self.ap[k], self.bufs)

    def rr(self, pat, **kw):
        return View(self.ap.rearrange(pat, **kw), self.bufs)

    def bc(self, shape):
        return View(self.ap.to_broadcast(list(shape)), self.bufs)

    def us(self, ax):
        return View(self.ap.unsqueeze(ax), self.bufs)


_TILE_ID = [0]


class Tile:
    def __init__(self, P, name, shape, dt=F32, psum=False, nb=1):
        _TILE_ID[0] += 1
        name = f"{name}_{_TILE_ID[0]}"
        if psum:
            self.t = P.es.enter_context(P.nc.psum_tensor("t_" + name, list(shape), dt))
        else:
            self.t = P.es.enter_context(P.nc.sbuf_tensor("t_" + name, list(shape), dt))
        self.bufs = [Buf() for _ in range(nb)]

    def __getitem__(self, k):
        return View(self.t[k], self.bufs)

    def part(self, i):
        return View(self.t[:], [self.bufs[i]])


def _bufs(*vs):
    out = []
    for v in vs:
        if isinstance(v, View):
            out.extend(v.bufs)
    return out


def _a(v):
    return v.ap if isinstance(v, View) else v


class Prog:
    def __init__(self, nc, es):
        self.nc = nc
        self.es = es
        self.e = {"pe": nc.tensor, "dve": nc.vector, "act": nc.scalar,
                  "pool": nc.gpsimd, "sp": nc.sync}
        self.sem = {k: es.enter_context(nc.semaphore("s_" + k))
                    for k in ("pe", "dve", "act", "pool")}
        self.cnt = {k: 0 for k in self.sem}
        self.seen = {k: {} for k in self.e}
        self.dsem = [es.enter_context(nc.semaphore(f"d{i}")) for i in range(NDS)]
        self.dcnt = [0] * NDS
        self.dnext = 0
        self.nins = 0
        self.dq = 0

    def _semobj(self, key):
        return self.dsem[key[1]] if isinstance(key, tuple) else self.sem[key]

    def _wait(self, eng, deps):
        best = {}
        for d in deps:
            if d is None:
                continue
            k, v = d
            if eng == "pe" and k == "pe":
                continue
            if best.get(k, 0) < v:
                best[k] = v
        seen = self.seen[eng]
        for k, v in best.items():
            if seen.get(k, 0) >= v:
                continue
            if not isinstance(k, tuple) and v > self.cnt[k]:
                raise RuntimeError(f"wait on un-issued inc {k} {v} > {self.cnt[k]}")
            self.e[eng].wait_ge(self._semobj(k), v)
            seen[k] = v

    def _deps(self, reads, writes):
        deps = []
        for b in reads:
            deps.append(b.w)
        for b in writes:
            deps.append(b.w)
            deps.extend(b.r.items())
        return deps

    def _mark(self, tok, reads, writes):
        k, v = tok
        for b in reads:
            if b.r.get(k, 0) < v:
                b.r[k] = v
        for b in writes:
            b.w = tok
            b.r = {}

    def op(self, eng, reads, writes, fn, inc=True):
        self._wait(eng, self._deps(reads, writes))
        ins = fn(self.e[eng])
        self.nins += 1
        if inc:
            self.cnt[eng] += 1
            ins.then_inc(self.sem[eng], 1)
            tok = (eng, self.cnt[eng])
        else:
            tok = (eng, self.cnt[eng] + 1)
        self._mark(tok, reads, writes)
        return ins

    def dma(self, out, in_, q=None, **kw):
        if q is None:
            q = "sp"
        if _a(out).dtype != _a(in_).dtype:
            q = "pool"
        reads, writes = _bufs(in_), _bufs(out)
        i = self.dnext
        self.dnext = (self.dnext + 1) % NDS
        deps = self._deps(reads, writes)
        if self.dcnt[i]:
            deps.append((("d", i), self.dcnt[i]))
        self._wait(q, deps)
        ins = self.e[q].dma_start(out=_a(out), in_=_a(in_), **kw)
        self.dcnt[i] += 16
        ins.then_inc(self.dsem[i], 16)
        self._mark((("d", i), self.dcnt[i]), reads, writes)
        self.nins += 1
        return ins

    def barrier(self):
        deps = [(k, self.cnt[k]) for k in self.sem if self.cnt[k]]
        deps += [(("d", i), self.dcnt[i]) for i in range(NDS) if self.dcnt[i]]
        for eng in ("pe", "dve", "act", "pool", "sp"):
            self._wait(eng, deps)

    def tt(self, out, a, b, op, eng="dve"):
        return self.op(eng, _bufs(a, b), _bufs(out),
                       lambda e: e.tensor_tensor(_a(out), _a(a), _a(b), op))

    def ts(self, out, a, s1, s2, op0, op1, eng="dve"):
        return self.op(eng, _bufs(a, s1, s2), _bufs(out),
                       lambda e: e.tensor_scalar(_a(out), _a(a), _a(s1), _a(s2), op0, op1))

    def stt(self, out, a, s, b, op0, op1, eng="dve"):
        return self.op(eng, _bufs(a, s, b), _bufs(out),
                       lambda e: e.scalar_tensor_tensor(_a(out), _a(a), _a(s), _a(b), op0, op1))

    def act(self, out, a, func, bias=0.0, scale=1.0, accum=None):
        if accum is None:
            return self.op("act", _bufs(a, bias, scale), _bufs(out),
                           lambda e: e.activation(_a(out), _a(a), func, bias=_a(bias), scale=_a(scale)))
        return self.op("act", _bufs(a, bias, scale), _bufs(out, accum),
                       lambda e: e.activation(_a(out), _a(a), func, bias=_a(bias), scale=_a(scale),
                                              accum_out=_a(accum)))

    def copy(self, out, a, eng="dve"):
        if eng == "act":
            return self.op("act", _bufs(a), _bufs(out), lambda e: e.copy(_a(out), _a(a)))
        return self.op(eng, _bufs(a), _bufs(out), lambda e: e.tensor_copy(_a(out), _a(a)))

    def red(self, out, a, op=ALU.add, axis=AX.X, eng="dve"):
        return self.op(eng, _bufs(a), _bufs(out),
                       lambda e: e.tensor_reduce(_a(out), _a(a), axis, op))

    def recip(self, out, a):
        return self.op("dve", _bufs(a), _bufs(out), lambda e: e.reciprocal(_a(out), _a(a)))

    def memset(self, out, val, eng="dve"):
        return self.op(eng, [], _bufs(out), lambda e: e.memset(_a(out), val))

    def mm(self, out, lhsT, rhs, start=True, stop=True, inc=None):
        return self.op("pe", _bufs(lhsT, rhs), _bufs(out),
                       lambda e: e.matmul(_a(out), _a(lhsT), _a(rhs), start=start, stop=stop),
                       inc=(stop if inc is None else (inc or stop)))

    def tr(self, out, a, ident):
        return self.op("pe", _bufs(a, ident), _bufs(out),
                       lambda e: e.transpose(_a(out), _a(a), _a(ident)))


class Ring:
    def __init__(self, P, name, shape, n, dt=F32):
        self.tiles = [Tile(P, f"{name}{i}", shape, dt) for i in range(n)]
        self.i = 0

    def next(self):
        t = self.tiles[self.i]
        self.i = (self.i + 1) % len(self.tiles)
        return t


def DV(ap):
    return View(ap, [])


class Launch:
    def __init__(self, ncores):
        self.nc = bass.Bass("TRN2", target_bir_lowering=False)
        self.ncores = ncores
        self.in_maps = [{} for _ in range(ncores)]

    def inp(self, name, arrs):
        if not isinstance(arrs, list):
            arrs = [arrs] * self.ncores
        for i in range(self.ncores):
            self.in_maps[i][name] = np.ascontiguousarray(arrs[i], dtype=np.float32)
        return self.nc.dram_tensor(name, list(arrs[0].shape), F32, kind="ExternalInput").ap()

    def out(self, name, shape):
        return self.nc.dram_tensor(name, list(shape), F32, kind="ExternalOutput").ap()

    def scratch(self, name, shape):
        return self.nc.dram_tensor(name, list(shape), F32).ap()

    def run(self):
        res = run_bass_kernel_spmd(self.nc, self.in_maps, core_ids=list(range(self.ncores)))
        return res.results


class Env:
    def __init__(self, P, C):
        self.P = P
        self.ps = [Tile(P, f"ps{i}", [128, 512], psum=True) for i in range(8)]
        self.ident = Tile(P, "ident", [128, 128])
        self.jrev = Tile(P, "jrev", [128, 128])
        self.rot = Tile(P, "rot", [128, 128])
        self.tri = [Tile(P, f"tri{e}", [128, 128]) for e in range(2)]
        self.mask4 = Tile(P, "mask4", [128, 4])
        self.ones = Tile(P, "ones", [128, 128])
        self.st = Tile(P, "st", [128, 16])
        self.junk = Tile(P, "junk", [128, 1024])
        self.cs = Tile(P, "cs", [128, 8, 2])
        self.csb = Tile(P, "csb", [128, 16, 128])
        self.mods = [[Tile(P, f"mod{v}_{t}", [128, 1024]) for t in range(2)] for v in range(3)]
        P.dma(self.ident[:], DV(C["ident"]))
        P.dma(self.jrev[:], DV(C["jrev"]))
        P.dma(self.rot[:], DV(C["rot"]))
        P.dma(self.tri[0][:], DV(C["tri0"]))
        P.dma(self.tri[1][:], DV(C["tri1"]))
        P.dma(self.mask4[:], DV(C["mask4"]))
        P.memset(self.ones[:], 1.0)
        self.onesr = Tile(P, "onesr", [128, 128], dt=F32R)
        P.copy(self.onesr[:], self.ones[:])
        P.dma(self.cs[:], DV(C["cvec"]))
        P.act(self.cs[:], self.cs[:], AF.Silu)
        P.copy(self.csb[:], self.cs[:].rr("p k t -> p (k t)").us(2).bc([128, 16, 128]))


class Scope:
    def __init__(self, P):
        self.P = P

    def __enter__(self):
        self.P.barrier()
        self.old = self.P.es
        self.stack = ExitStack()
        self.stack.__enter__()
        self.P.es = self.stack
        return self

    def __exit__(self, *a):
        self.P.barrier()
        self.P.es = self.old
        return self.stack.__exit__(*a)


def emit_mods(P, E, W, l, s, weight):
    with Scope(P):
        nw = Tile(P, "nw", [128, 2, 1024])
        adab = Tile(P, "adab", [1, 1024])
        adaw = Ring(P, "adaw", [128, 8, 128], 3)
        P.dma(nw[:], DV(W["norm_w"][l, 2 * s:2 * s + 2, :].partition_broadcast(128)))
        for v in range(3):
            m = 3 * s + v
            P.dma(adab[:], DV(W["ada_b"][l:l + 1, m * D:(m + 1) * D]))
            for q in range(8):
                wt = adaw.next()
                c0 = m * D + q * 128
                P.dma(wt[:], DV(W["ada_w"][l, :, c0:c0 + 128].rearrange("(kc p) n -> p kc n", p=128)))
                cs = slice(q * 128, (q + 1) * 128)
                for t in range(2):
                    ps = E.ps[(2 * q + t) % 8][:, 0:128]
                    for kc in range(8):
                        P.mm(ps, E.csb[:, kc * 2 + t, :], wt[:, kc, :], start=(kc == 0), stop=False)
                    P.mm(ps, E.ones[0:1, :], adab[0:1, cs], start=False, stop=True)
                    if v == 0:
                        P.copy(E.mods[1][t][:, cs], ps)
                    elif v == 1:
                        P.stt(E.mods[0][t][:, cs], ps, 1.0, nw[:, 0, cs], ALU.add, ALU.mult)
                    else:
                        P.stt(E.mods[2][t][:, cs], ps, float(weight), nw[:, 1, cs], ALU.mult, ALU.mult)


def emit_transpose(P, E, dst, src, ncol, npart=128):
    for c0 in range(0, ncol, 4):
        n = min(4, ncol - c0)
        ps = E.ps[6 + (c0 // 4) % 2]
        for c in range(n):
            P.tr(ps[:, c * 128:c * 128 + npart], src[0:npart, (c0 + c) * 128:(c0 + c + 1) * 128],
                 E.ident[0:npart, 0:npart])
        P.copy(dst[:, c0:c0 + n, :], ps[:, 0:n * 128].rr("p (c t) -> p c t", c=n)[:, :, 0:npart])


def emit_prenorm(P, E, h, x, typ):
    st = E.st
    P.act(E.junk[:, 0:1024], x, AF.Square, accum=st[:, 0:1])
    P.act(st[:, 1:2], st[:, 0:1], AF.Sqrt, bias=1e-6, scale=1.0 / 1024)
    P.recip(st[:, 2:3], st[:, 1:2])
    P.stt(h, x, st[:, 2:3], E.mods[0][typ][:], ALU.mult, ALU.mult)
    P.tt(h, h, E.mods[1][typ][:], ALU.add)


def emit_postadd(P, E, tmp, xv, ys, typ):
    st = E.st
    for hf in range(2):
        P.act(E.junk[:, 0:512], ys[hf], AF.Square, accum=st[:, 4 + hf:5 + hf])
    P.tt(st[:, 6:7], st[:, 4:5], st[:, 5:6], ALU.add)
    P.act(st[:, 7:8], st[:, 6:7], AF.Sqrt, bias=1e-6, scale=1.0 / 1024)
    P.recip(st[:, 8:9], st[:, 7:8])
    for hf in range(2):
        cs = slice(hf * 512, (hf + 1) * 512)
        P.stt(tmp[:], ys[hf], st[:, 8:9], E.mods[2][typ][:, cs], ALU.mult, ALU.mult)
        P.tt(xv[:, cs], xv[:, cs], tmp[:], ALU.add)


def groups_of(NT):
    g = [[0, 1]]
    for t0 in range(2, NT, 4):
        g.append(list(range(t0, min(t0 + 4, NT))))
    return g


def typ_of(ti):
    return 1 if ti < 2 else 0


def rows(ap, ti, n=128):
    return DV(ap[ti * 128:ti * 128 + n])


def st_ffn(P, E, x_in, x_out, w_in, w_out, NT, out_lat=None):
    with Scope(P):
        xg = Tile(P, "xg", [128, 4, 1024], nb=4)
        h = Tile(P, "hh", [128, 1024])
        hT = Tile(P, "hT", [128, 8, 512], dt=F32R)
        actT = Tile(P, "actT", [128, 22, 512], dt=F32R)
        sgr = Ring(P, "sg", [128, 512], 2)
        wir = Ring(P, "wi", [128, 8, 256], 2, dt=F32R)
        wor = Ring(P, "wo", [128, 1024], 2, dt=F32R)
        tmp = Tile(P, "tmpy", [128, 512])
        for tl in groups_of(NT):
            if out_lat is not None and tl[0] < 2:
                continue
            T = len(tl) * 128
            for i, ti in enumerate(tl):
                xv = xg.part(i)[:, i, :]
                P.dma(xv, rows(x_in, ti))
                emit_prenorm(P, E, h[:], xv, typ_of(ti))
                emit_transpose(P, E, hT[:, :, i * 128:(i + 1) * 128], h[:], 8)
            for j in range(22):
                wi = wir.next()
                P.dma(wi[:], DV(w_in[j]), q="pool")
                pg = E.ps[j % 2]
                pu = E.ps[2 + j % 2]
                for kc in range(8):
                    P.mm(pg[:, 0:T], wi[:, kc, 0:128], hT[:, kc, 0:T], start=(kc == 0), stop=(kc == 7))
                for kc in range(8):
                    P.mm(pu[:, 0:T], wi[:, kc, 128:256], hT[:, kc, 0:T], start=(kc == 0), stop=(kc == 7))
                sg = sgr.next()
                P.act(sg[:, 0:T], pg[:, 0:T], AF.Silu)
                P.tt(actT[:, j, 0:T], sg[:, 0:T], pu[:, 0:T], ALU.mult)
            for j in range(22):
                wo = wor.next()
                P.dma(wo[:], DV(w_out[j * 128:(j + 1) * 128, :]), q="pool")
                for i in range(len(tl)):
                    for hf in range(2):
                        P.mm(E.ps[i * 2 + hf][:], actT[:, j, i * 128:(i + 1) * 128],
                             wo[:, hf * 512:(hf + 1) * 512], start=(j == 0), stop=(j == 21),
                             inc=(i == len(tl) - 1 and hf == 1))
            for i, ti in enumerate(tl):
                xv = xg.part(i)[:, i, :]
                emit_postadd(P, E, tmp, xv, [E.ps[i * 2][:], E.ps[i * 2 + 1][:]], typ_of(ti))
                if out_lat is not None:
                    P.dma(rows(out_lat, ti - 2), xv, q="sp")
                else:
                    P.dma(rows(x_out, ti), xv, q="sp")


def st_pre_even(P, E, X, W, j, S, NT):
    wb = W["ev_w_in"]
    with Scope(P):
        xt = Tile(P, "xt", [128, 1024])
        h = Tile(P, "hh", [128, 1024])
        hT = Tile(P, "hT", [128, 8, 512], dt=F32R)
        wr = Ring(P, "wblk", [128, 8, 128], 3, dt=F32R)
        w4r = Ring(P, "wblk4", [128, 4, 8, 128], 2, dt=F32R)
        wdt = Tile(P, "wdt", [128, 8, 128], dt=F32R)
        qsr = Ring(P, "qs", [128, 512], 3)
        tmr = Ring(P, "tm", [128, 512], 2)
        cs = Tile(P, "cosg", [128, 512])
        sn = Tile(P, "sing", [128, 512])
        dtb = Tile(P, "dtb", [128, 32])
        dtt = Ring(P, "dtt", [128, 32], 2)
        P.dma(dtb[:], DV(W["ssd_dt_bias"][j:j + 1, :].partition_broadcast(128)).rr("p o n -> p (o n)"))
        P.dma(wdt[:], DV(wb[j, 44]), q="pool")
        for tl in groups_of(NT):
            T = len(tl) * 128
            t0 = tl[0] * 128
            lat = tl[0] >= 2
            for i, ti in enumerate(tl):
                P.dma(xt[:], rows(X, ti))
                emit_prenorm(P, E, h[:], xt[:], typ_of(ti))
                emit_transpose(P, E, hT[:, :, i * 128:(i + 1) * 128], h[:], 8)
            if lat:
                P.dma(cs[:, 0:T], DV(W["cosT"][:, t0 - CTX:t0 - CTX + T]))
                P.dma(sn[:, 0:T], DV(W["sinT"][:, t0 - CTX:t0 - CTX + T]))
            for blk in list(range(16)) + list(range(32, 44)):
                wt = wr.next()
                P.dma(wt[:], DV(wb[j, blk]), q="pool")
                ps = E.ps[blk % 2]
                for kc in range(8):
                    P.mm(ps[:, 0:T], wt[:, kc, :], hT[:, kc, 0:T], start=(kc == 0), stop=(kc == 7))
                qs = qsr.next()
                P.copy(qs[:, 0:T], ps[:, 0:T], eng="act")
                if blk < 16:
                    if lat:
                        ps2 = E.ps[2 + blk % 2]
                        P.mm(ps2[:, 0:T], E.rot[:], qs[:, 0:T])
                        tm = tmr.next()
                        P.tt(tm[:, 0:T], ps2[:, 0:T], sn[:, 0:T], ALU.mult)
                        P.tt(qs[:, 0:T], qs[:, 0:T], cs[:, 0:T], ALU.mult)
                        P.tt(qs[:, 0:T], qs[:, 0:T], tm[:, 0:T], ALU.add)
                    dst = S["QT"] if blk < 8 else S["KT"]
                    P.dma(DV(dst[blk % 8, :, t0:t0 + T]), qs[:, 0:T], q="sp")
                else:
                    P.dma(DV(S["XBCT"][blk - 32, :, t0:t0 + T]), qs[:, 0:T], q="sp")
            for cb in range(4):
                w4 = w4r.next()
                P.dma(w4[:], DV(wb[j, 16 + cb * 4:16 + cb * 4 + 4].rearrange("b p k n -> p b k n")), q="pool")
                for i, ti in enumerate(tl):
                    ps = E.ps[4 + i % 2]
                    for kc in range(8):
                        P.mm(ps[:].rr("p (b n) -> p b n", b=4), hT[:, kc, i * 128:(i + 1) * 128],
                             w4[:, :, kc, :], start=(kc == 0), stop=(kc == 7))
                    qs = qsr.next()
                    P.copy(qs[:], ps[:], eng="act")
                    dst = S["V"] if cb < 2 else S["Z"]
                    P.dma(DV(dst[ti * 128:(ti + 1) * 128, (cb % 2) * 512:(cb % 2 + 1) * 512]), qs[:], q="sp")
            for i, ti in enumerate(tl):
                ps = E.ps[4 + i % 2]
                for kc in range(8):
                    P.mm(ps[:, 0:32], hT[:, kc, i * 128:(i + 1) * 128], wdt[:, kc, 0:32],
                         start=(kc == 0), stop=(kc == 7))
                d = dtt.next()
                P.tt(d[:], ps[:, 0:32], dtb[:], ALU.add)
                P.act(d[:], d[:], AF.Exp)
                P.act(d[:], d[:], AF.Ln, bias=1.0)
                P.dma(rows(S["DT"], ti), d[:], q="sp")


def st_attn(P, E, W, j, l, S, NT):
    lam_init = 0.8 - 0.6 * math.exp(-0.3 * l)
    NTOK = NT * 128
    with Scope(P):
        KTs = Tile(P, "KTs", [128, NTOK], dt=F32R)
        Vs = Tile(P, "Vs", [128, NT, 128], dt=F32R)
        Qr = Ring(P, "Qs", [128, 512], 2, dt=F32R)
        er = Ring(P, "eexp", [128, 512], 4, dt=F32R)
        lamt = Tile(P, "lamt", [128, 4, 64])
        lams = Tile(P, "lams", [128, 8])
        sub = Tile(P, "subw", [128, 2])
        a0 = Tile(P, "a0", [128, 512])
        a1 = Tile(P, "a1", [128, 512])
        rz = Tile(P, "rz", [128, 512])
        zacc = [Tile(P, f"zacc{m}", [128, 512]) for m in range(2)]
        P.dma(lamt[:], DV(W["da_lambda"][j].partition_broadcast(128)))
        P.tt(lamt[:, 0, :], lamt[:, 0, :], lamt[:, 1, :], ALU.mult)
        P.tt(lamt[:, 2, :], lamt[:, 2, :], lamt[:, 3, :], ALU.mult)
        P.red(lams[:, 0:1], lamt[:, 0, :])
        P.red(lams[:, 1:2], lamt[:, 2, :])
        P.act(lams[:, 2:4], lams[:, 0:2], AF.Exp)
        P.tt(lams[:, 4:5], lams[:, 3:4], lams[:, 2:3], ALU.subtract)
        P.ts(lams[:, 5:6], lams[:, 4:5], -lam_init, 1.0, ALU.add, ALU.mult)
        P.dma(sub[:, 0:1], DV(W["da_subln"][j].rearrange("(p o) -> p o", o=1)))
        P.ts(sub[:, 1:2], sub[:, 0:1], 1.0 - lam_init, 0.0, ALU.mult, ALU.add)
        nlam = lams[:, 5:6]
        for hd in range(8):
            P.dma(KTs[:], DV(S["KT"][hd]), q="pool")
            P.dma(Vs[:], DV(S["V"][:, hd * 128:(hd + 1) * 128].rearrange("(t p) e -> p t e", p=128)), q="pool")
            blocks = [(0, 256, 2)] + [(t0, min(512, NTOK - t0), NT) for t0 in range(256, NTOK, 512)]
            for (t0, T, nk) in blocks:
                Qs = Qr.next()
                P.dma(Qs[:, 0:T], DV(S["QT"][hd, :, t0:t0 + T]), q="pool")
                O = [E.ps[4], E.ps[5]]
                Z = [E.ps[6], E.ps[7]]
                tiles = [(kt, m) for kt in range(nk) for m in range(2)]
                LAG = 2
                for idx in range(len(tiles) + LAG):
                    if idx < len(tiles):
                        kt, m = tiles[idx]
                        P.mm(E.ps[idx % 4][:, 0:T], KTs[m * 64:(m + 1) * 64, kt * 128:(kt + 1) * 128],
                             Qs[m * 64:(m + 1) * 64, 0:T])
                    jx = idx - LAG
                    if jx >= 0:
                        kt, m = tiles[jx]
                        ee = er.next()
                        P.act(ee[:, 0:T], E.ps[jx % 4][:, 0:T], AF.Exp, scale=0.125)
                        P.mm(O[m][:, 0:T], Vs[:, kt, :], ee[:, 0:T], start=(kt == 0), stop=(kt == nk - 1), inc=True)
                        ee32 = View(ee[:, 0:T].ap.bitcast(F32), ee.bufs)
                        if kt == 0:
                            P.copy(zacc[m][:, 0:T], ee32)
                        else:
                            P.tt(zacc[m][:, 0:T], zacc[m][:, 0:T], ee32, ALU.add)
                for m in range(2):
                    P.mm(Z[m][:, 0:T], E.ones[:], zacc[m][:, 0:T])
                P.recip(rz[:, 0:T], Z[0][:, 0:T])
                P.tt(a0[:, 0:T], O[0][:, 0:T], rz[:, 0:T], ALU.mult)
                P.recip(rz[:, 0:T], Z[1][:, 0:T])
                P.tt(a1[:, 0:T], O[1][:, 0:T], rz[:, 0:T], ALU.mult)
                P.stt(a0[:, 0:T], a1[:, 0:T], nlam, a0[:, 0:T], ALU.mult, ALU.add)
                P.tt(a1[:, 0:T], a0[:, 0:T], a0[:, 0:T], ALU.mult)
                P.mm(E.ps[0][:, 0:T], E.ones[:], a1[:, 0:T])
                P.act(rz[:, 0:T], E.ps[0][:, 0:T], AF.Sqrt, bias=1e-5, scale=1.0 / 128)
                P.recip(rz[:, 0:T], rz[:, 0:T])
                P.stt(a0[:, 0:T], a0[:, 0:T], sub[:, 1:2], rz[:, 0:T], ALU.mult, ALU.mult)
                P.dma(DV(S["MIXT"][hd, :, t0:t0 + T]), a0[:, 0:T], q="sp")


def st_ssd(P, E, W, j, S, NT):
    NTOK = NT * 128
    with Scope(P):
        cw = Tile(P, "cw", [128, 12, 6])
        xin = Ring(P, "cxin", [128, 132], 3)
        acc = Ring(P, "cacc", [128, 128], 3)
        tk = Tile(P, "tk", [128, 1280])
        P.dma(cw[:], DV(W["ssd_conv"][j]))
        for c in range(NT):
            lz = c in (0, 2)
            rz = c in (1, NT - 1)
            for blk in range(12):
                xi = xin.next()
                lo = 2 if lz else 0
                hi = 130 if rz else 132
                if lz:
                    P.memset(xi[:, 0:2], 0.0)
                if rz:
                    P.memset(xi[:, 130:132], 0.0)
                P.dma(xi[:, lo:hi], DV(S["XBCT"][blk, :, c * 128 - 2 + lo:c * 128 - 2 + hi]))
                a = acc.next()
                P.ts(a[:], xi[:, 0:128], cw[:, blk, 0:1], cw[:, blk, 5:6], ALU.mult, ALU.add)
                for q in range(1, 5):
                    P.stt(a[:], xi[:, q:q + 128], cw[:, blk, q:q + 1], a[:], ALU.mult, ALU.add)
                P.act(a[:], a[:], AF.Silu)
                if blk >= 8:
                    P.dma(DV(S["BCT"][blk - 8, :, c * 128:(c + 1) * 128]), a[:], q="sp")
                if blk < 10:
                    ps = E.ps[blk // 4]
                    P.tr(ps[:, (blk % 4) * 128:(blk % 4 + 1) * 128], a[:], E.ident[:])
            P.copy(tk[:, 0:512], E.ps[0][:])
            P.copy(tk[:, 512:1024], E.ps[1][:], eng="act")
            P.copy(tk[:, 1024:1280], E.ps[2][:, 0:256])
            P.dma(rows(S["XS"], c), tk[:, 0:1024], q="sp")
            P.dma(rows(S["BTOK"], c), tk[:, 1024:1280], q="sp")
    for e in range(2):
        with Scope(P):
            H = Tile(P, "H", [128, 16, 64])
            Abc = Tile(P, "Abc", [128, 16])
            Dbc = Tile(P, "Dbc", [128, 16])
            xs = Tile(P, "xs", [128, 16, 64])
            yf = Tile(P, "yf", [128, 16, 64])
            dt16 = Tile(P, "dt16", [128, 16])
            bct = Tile(P, "bct", [128, 4, 128])
            btok = Tile(P, "btok", [128, 256])
            sm = Tile(P, "sm", [128, 8, 16])
            dtAb = Tile(P, "dtAb", [128, 16, 128])
            xdt = Tile(P, "xdt", [128, 16, 64])
            xw = Tile(P, "xw", [128, 16, 64])
            cbm = [Tile(P, f"cbm{g}", [128, 128]) for g in range(2)]
            argr = Ring(P, "arg", [128, 128], 3)
            mtr = Ring(P, "mt", [128, 128], 3)
            y = Tile(P, "ysb", [128, 16, 64])
            P.memset(H[:], 0.0)
            P.dma(Abc[:], DV(W["ssd_a_log"][j:j + 1, e * 16:(e + 1) * 16].partition_broadcast(128)).rr("p o n -> p (o n)"))
            P.act(Abc[:], Abc[:], AF.Exp)
            P.ts(Abc[:], Abc[:], -1.0, 0.0, ALU.mult, ALU.add)
            P.dma(Dbc[:], DV(W["ssd_d"][j:j + 1, :].partition_broadcast(128)).rr("p o n -> p (o n)"))
            order = list(range(NT)) if e == 0 else [1, 0] + list(range(NT - 1, 1, -1))
            tri = E.tri[e]
            for c in order:
                P.dma(xs[:].rr("p h d -> p (h d)"), rows(S["XS"], c))
                P.dma(dt16[:], DV(S["DT"][c * 128:(c + 1) * 128, e * 16:(e + 1) * 16]))
                P.dma(bct[:], DV(S["BCT"][:, :, c * 128:(c + 1) * 128].rearrange("b p t -> p b t")))
                P.dma(btok[:], rows(S["BTOK"], c))
                if e == 1:
                    P.dma(yf[:].rr("p h d -> p (h d)"), rows(S["YS"], c))
                dtA = sm[:, 0, :]
                P.tt(dtA, dt16[:], Abc[:], ALU.mult)
                P.copy(dtAb[:], dtA.us(2).bc([128, 16, 128]))
                P.mm(E.ps[7][:, 0:16], tri[:], dtA)
                P.mm(E.ps[7][:, 16:32], E.ones[:], dtA)
                P.copy(sm[:, 1:3, :], E.ps[7][:, 0:32].rr("p (a h) -> p a h", a=2))
                P.act(sm[:, 3:5, :], sm[:, 1:3, :], AF.Exp)
                P.tt(sm[:, 5, :], sm[:, 2, :], sm[:, 1, :], ALU.subtract)
                P.act(sm[:, 5, :], sm[:, 5, :], AF.Exp)
                P.tt(sm[:, 6, :], sm[:, 5, :], dt16[:], ALU.mult)
                P.tt(xdt[:], xs[:], dt16[:].us(2).bc([128, 16, 64]), ALU.mult)
                P.tt(xw[:], xs[:], sm[:, 6, :].us(2).bc([128, 16, 64]), ALU.mult)
                for g in range(2):
                    P.mm(E.ps[6][:, g * 128:(g + 1) * 128], bct[:, g, :], bct[:, 2 + g, :])
                    P.tt(cbm[g][:], E.ps[6][:, g * 128:(g + 1) * 128], tri[:], ALU.mult)
                for hd in range(16):
                    g = hd // 8
                    pr = E.ps[hd % 2]
                    P.mm(pr[:, 0:128], dtAb[:, hd, :], tri[:])
                    ar = argr.next()
                    P.ts(ar[:], pr[:, 0:128], sm[:, 1, hd:hd + 1], 0.0, ALU.subtract, ALU.min)
                    P.act(ar[:], ar[:], AF.Exp)
                    mt = mtr.next()
                    P.tt(mt[:], ar[:], cbm[g][:], ALU.mult, eng="pool")
                    P.mm(E.ps[2 + g][:, (hd % 8) * 64:(hd % 8 + 1) * 64], mt[:], xdt[:, hd, :])
                    P.mm(E.ps[4 + g][:, (hd % 8) * 64:(hd % 8 + 1) * 64], bct[:, 2 + g, :], H[:, hd, :])
                for g in range(2):
                    hs = slice(g * 8, (g + 1) * 8)
                    P.tt(y[:, hs, :], E.ps[4 + g][:].rr("p (h d) -> p h d", h=8),
                         sm[:, 3, hs].us(2).bc([128, 8, 64]), ALU.mult)
                    P.tt(y[:, hs, :], y[:, hs, :], E.ps[2 + g][:].rr("p (h d) -> p h d", h=8), ALU.add)
                if e == 1:
                    P.tt(y[:], y[:], yf[:], ALU.add)
                    P.tt(yf[:], xs[:], Dbc[:].us(2).bc([128, 16, 64]), ALU.mult)
                    P.tt(y[:], y[:], yf[:], ALU.add)
                P.dma(rows(S["YS"], c), y[:].rr("p h d -> p (h d)"), q="sp")
                for g in range(2):
                    P.mm(E.ps[6 + g][:], btok[:, g * 128:(g + 1) * 128],
                         xw[:, g * 8:(g + 1) * 8, :].rr("p h d -> p (h d)"))
                P.tt(H[:], H[:], sm[:, 4, :].us(2).bc([128, 16, 64]), ALU.mult)
                for g in range(2):
                    hs = slice(g * 8, (g + 1) * 8)
                    P.tt(H[:, hs, :], H[:, hs, :], E.ps[6 + g][:].rr("p (h d) -> p h d", h=8), ALU.add)


def st_post_even(P, E, X, W, j, S, NT):
    with Scope(P):
        xg = Tile(P, "xg", [128, 4, 1024], nb=4)
        ys = Tile(P, "ysb", [128, 1024])
        zs = Tile(P, "zsb", [128, 1024])
        nrm = Tile(P, "nrmbc", [128, 1024])
        fT = Tile(P, "fT", [128, 16, 512], dt=F32R)
        wor = Ring(P, "wo", [128, 1024], 3, dt=F32R)
        tmp = Tile(P, "tmpy", [128, 512])
        P.dma(nrm[:], DV(W["ssd_norm"][j:j + 1, :].partition_broadcast(128)).rr("p o n -> p (o n)"))
        st = E.st
        for tl in groups_of(NT):
            T = len(tl) * 128
            t0 = tl[0] * 128
            for i, ti in enumerate(tl):
                P.dma(xg.part(i)[:, i, :], rows(X, ti))
                P.dma(ys[:], rows(S["YS"], ti))
                P.dma(zs[:], rows(S["Z"], ti))
                P.act(zs[:], zs[:], AF.Silu)
                P.tt(ys[:], ys[:], zs[:], ALU.mult)
                for gg in range(2):
                    cs = slice(gg * 512, (gg + 1) * 512)
                    P.act(E.junk[:, 0:512], ys[:, cs], AF.Square, accum=st[:, 10:11])
                    P.act(st[:, 11:12], st[:, 10:11], AF.Sqrt, bias=1e-5, scale=1.0 / 512)
                    P.recip(st[:, 12:13], st[:, 11:12])
                    P.stt(ys[:, cs], ys[:, cs], st[:, 12:13], nrm[:, cs], ALU.mult, ALU.mult)
                emit_transpose(P, E, fT[:, 8:16, i * 128:(i + 1) * 128], ys[:], 8)
            P.dma(fT[:, 0:8, 0:T], DV(S["MIXT"][:, :, t0:t0 + T].rearrange("b p t -> p b t")), q="pool")
            for blk in range(16):
                wo = wor.next()
                P.dma(wo[:], DV(W["ev_w_out"][j, blk * 128:(blk + 1) * 128, :]), q="pool")
                for i in range(len(tl)):
                    for hf in range(2):
                        P.mm(E.ps[i * 2 + hf][:], fT[:, blk, i * 128:(i + 1) * 128],
                             wo[:, hf * 512:(hf + 1) * 512], start=(blk == 0), stop=(blk == 15),
                             inc=(i == len(tl) - 1 and hf == 1))
            for i, ti in enumerate(tl):
                xv = xg.part(i)[:, i, :]
                emit_postadd(P, E, tmp, xv, [E.ps[i * 2][:], E.ps[i * 2 + 1][:]], typ_of(ti))
                P.dma(rows(X, ti), xv, q="sp")


REC = 386


def st_pre_odd(P, E, X, S, NT):
    with Scope(P):
        xt = Tile(P, "xt", [128, 1024])
        h = Tile(P, "hh", [128, 1024])
        hT = Ring(P, "hT1", [128, 8, 128], 2)
        for ti in range(NT):
            P.dma(xt[:], rows(X, ti))
            emit_prenorm(P, E, h[:], xt[:], typ_of(ti))
            t = hT.next()
            emit_transpose(P, E, t[:], h[:], 8)
            P.dma(DV(S["HT"][:, :, ti * 128:(ti + 1) * 128].rearrange("k p t -> p k t")), t[:], q="sp")


def rev_step_off(ti, NT):
    if ti < 2:
        return (1 - ti) * 128
    return CTX + (NT * 128 - (ti + 1) * 128)


def st_prep_odd(P, E, W, j, S, NT, use_vgate):
    NTOK = NT * 128
    with Scope(P):
        hx = Tile(P, "hx", [128, 8, 130])
        xx = Tile(P, "xx", [128, 8, 128])
        xmr = Ring(P, "xm", [128, 8, 128], 2, dt=F32R)
        mu = Tile(P, "mu", [128, 6, 8])
        wkr = Ring(P, "wk", [128, 1024], 4, dt=F32R)
        w1 = Tile(P, "w1", [128, 2, 8, 64], dt=F32R)
        w2 = Tile(P, "w2", [64, 2, 1024], dt=F32R)
        a1 = Tile(P, "a1", [128, 2, 8, 64], dt=F32R)
        a2 = Tile(P, "a2", [64, 2, 1024], dt=F32R)
        g1 = Tile(P, "g1", [128, 8, 160], dt=F32R)
        g2a = Tile(P, "g2a", [128, 1024], dt=F32R)
        g2b = Tile(P, "g2b", [32, 1024], dt=F32R)
        w0 = Tile(P, "w0", [128, 2, 1024])
        a0 = Tile(P, "a0", [128, 2, 1024])
        kkv = Tile(P, "kkv", [128, 3, 1024])
        lo = Ring(P, "lo", [128, 128], 3, dt=F32R)
        r = Tile(P, "r", [128, 1024])
        k = Tile(P, "k", [128, 1024])
        v = Tile(P, "v", [128, 1024])
        g = Tile(P, "g", [128, 1024])
        kk = Tile(P, "kk", [128, 1024])
        we = Tile(P, "we", [128, 1024])
        ae = Tile(P, "ae", [128, 1024])
        kd = Tile(P, "kd", [128, 1024])
        kds = Tile(P, "kds", [128, 1024])
        t1 = Tile(P, "t1", [128, 1024])
        t2 = Tile(P, "t2", [128, 1024])
        sm = Tile(P, "sm16", [128, 8, 16])
        rvr = Ring(P, "rv", [128, 1024], 2)
        if use_vgate:
            v1 = Tile(P, "v1", [128, 8, 32], dt=F32R)
            v2 = Tile(P, "v2", [32, 1024], dt=F32R)
            v0 = Tile(P, "v0", [128, 1024])
            vf = Tile(P, "vf", [128, 1024])
            P.dma(v1[:], DV(W["tm_v1"][0].rearrange("(kc p) r -> p kc r", p=128)), q="pool")
            P.dma(v2[:], DV(W["tm_v2"][0]))
            P.dma(v0[:], DV(W["tm_v0"][0].partition_broadcast(128)))
        P.dma(mu[:], DV(W["tm_mu"][j].rearrange("i (kc p) -> p i kc", p=128)), allow_slow_non_contiguous=True)
        for e in range(2):
            P.dma(w1[:, e], DV(W["tm_w1"][j, e].rearrange("(kc p) r -> p kc r", p=128)), q="pool")
            P.dma(a1[:, e], DV(W["tm_a1"][j, e].rearrange("(kc p) r -> p kc r", p=128)), q="pool")
            P.dma(w2[:, e], DV(W["tm_w2"][j, e]))
            P.dma(a2[:, e], DV(W["tm_a2"][j, e]))
        P.dma(g1[:], DV(W["tm_g1"][j].rearrange("(kc p) r -> p kc r", p=128)), q="pool")
        P.dma(g2a[:], DV(W["tm_g2"][j, 0:128, :]))
        P.dma(g2b[:], DV(W["tm_g2"][j, 128:160, :]))
        P.dma(w0[:], DV(W["tm_w0"][j].partition_broadcast(128)))
        P.dma(a0[:], DV(W["tm_a0"][j].partition_broadcast(128)))
        P.dma(kkv[:, 0], DV(W["tm_k_k"][j].partition_broadcast(128)))
        P.dma(kkv[:, 1], DV(W["tm_k_a"][j].partition_broadcast(128)))
        P.dma(kkv[:, 2], DV(W["tm_r_k"][j].partition_broadcast(128)))

        def mix(i):
            xm = xmr.next()
            P.tt(xm[:], xx[:], mu[:, i, :].us(2).bc([128, 8, 128]), ALU.mult)
            P.tt(xm[:], xm[:], hx[:, :, 1:129], ALU.add)
            return xm

        def proj(xm, wname, dst):
            for kc in range(8):
                wt = wkr.next()
                P.dma(wt[:], DV(W[wname][j, kc * 128:(kc + 1) * 128, :]), q="pool")
                for hf in range(2):
                    P.mm(E.ps[hf][:], xm[:, kc, :], wt[:, hf * 512:(hf + 1) * 512],
                         start=(kc == 0), stop=(kc == 7), inc=(hf == 1))
            P.copy(dst[:, 0:512], E.ps[0][:])
            P.copy(dst[:, 512:1024], E.ps[1][:], eng="act")

        def lora(xm, w1v, R, func):
            ps = E.ps[2]
            for kc in range(8):
                P.mm(ps[0:R, 0:128], w1v(kc), xm[:, kc, :], start=(kc == 0), stop=(kc == 7))
            t = lo.next()
            if func is None:
                P.copy(t[0:R, :], ps[0:R, 0:128])
            else:
                P.act(t[0:R, :], ps[0:R, 0:128], func)
            return t

        def write_rec(e, src3, q0, n, ti):
            if e == 0:
                so = ti * 128
            else:
                so = rev_step_off(ti, NT)
            base = S["SC"][e]
            dst = bass.AP(base.tensor, base.offset + so * REC + q0, [[REC, 128], [NTOK * REC, 16], [1, n]])
            for h0 in range(0, 16, 4):
                d4 = bass.AP(base.tensor, base.offset + so * REC + q0 + h0 * NTOK * REC,
                             [[REC, 128], [NTOK * REC, 4], [1, n]])
                if n == 1:
                    P.dma(DV(d4), src3[:, h0:h0 + 4, :], q="sp", allow_slow_non_contiguous=True)
                else:
                    P.dma(DV(d4), src3[:, h0:h0 + 4, :], q="sp")

        def emit_q(e, src, q0, ti):
            if e == 0:
                write_rec(0, src[:].rr("p (h d) -> p h d", h=16), q0, 64, ti)
            else:
                rv = rvr.next()
                for hf in range(2):
                    P.mm(E.ps[4 + hf][:], E.jrev[:], src[:, hf * 512:(hf + 1) * 512])
                P.copy(rv[:, 0:512], E.ps[4][:])
                P.copy(rv[:, 512:1024], E.ps[5][:], eng="act")
                write_rec(1, rv[:].rr("p (h d) -> p h d", h=16), q0, 64, ti)

        def emit_s(e, src16, q0, ti):
            if e == 0:
                write_rec(0, src16.us(2), q0, 1, ti)
            else:
                rv = rvr.next()
                P.mm(E.ps[4][:, 0:16], E.jrev[:], src16)
                P.copy(rv[:, 0:16], E.ps[4][:, 0:16])
                write_rec(1, rv[:, 0:16].us(2), q0, 1, ti)

        for ti in range(NT):
            t0 = ti * 128
            lz = ti in (0, 2)
            rz = ti in (1, NT - 1)
            lo_c = 1 if lz else 0
            hi_c = 129 if rz else 130
            if lz:
                P.memset(hx[:, :, 0:1], 0.0)
            if rz:
                P.memset(hx[:, :, 129:130], 0.0)
            for kc in range(8):
                P.dma(hx[:, kc, lo_c:hi_c], DV(S["HT"][kc, :, t0 - 1 + lo_c:t0 - 1 + hi_c]))
            P.tt(xx[:], hx[:, :, 0:128], hx[:, :, 2:130], ALU.add)
            P.stt(xx[:], xx[:], 0.5, hx[:, :, 1:129], ALU.mult, ALU.subtract)
            proj(mix(0), "tm_w_r", r)
            proj(mix(2), "tm_w_k", k)
            xmv = mix(3)
            proj(xmv, "tm_w_v", v)
            if use_vgate:
                vl = lora(xmv, lambda kc: v1[:, kc, :], 32, None)
                for hf in range(2):
                    P.mm(E.ps[hf][:], vl[0:32, :], v2[:, hf * 512:(hf + 1) * 512])
                    cs = slice(hf * 512, (hf + 1) * 512)
                    P.tt(t1[:, cs], E.ps[hf][:], v0[:, cs], ALU.add)
                P.act(t1[:], t1[:], AF.Sigmoid)
                P.dma(vf[:], rows(S["VF"], ti))
                P.tt(vf[:], vf[:], v[:], ALU.subtract)
                P.tt(vf[:], vf[:], t1[:], ALU.mult)
                P.tt(v[:], v[:], vf[:], ALU.add)
            else:
                P.dma(rows(S["VF"], ti), v[:], q="sp")
            xmg = mix(5)
            for kc in range(8):
                P.mm(E.ps[2][:, 0:128], g1[:, kc, 0:128], xmg[:, kc, :], start=(kc == 0), stop=(kc == 7))
            for kc in range(8):
                P.mm(E.ps[3][0:32, 0:128], g1[:, kc, 128:160], xmg[:, kc, :], start=(kc == 0), stop=(kc == 7))
            ga = lo.next()
            gb = lo.next()
            P.act(ga[:], E.ps[2][:, 0:128], AF.Sigmoid)
            P.act(gb[0:32, :], E.ps[3][0:32, 0:128], AF.Sigmoid)
            for hf in range(2):
                cs = slice(hf * 512, (hf + 1) * 512)
                P.mm(E.ps[hf][:], ga[:], g2a[:, cs], start=True, stop=False)
                P.mm(E.ps[hf][:], gb[0:32, :], g2b[:, cs], start=False, stop=True)
            P.copy(g[:, 0:512], E.ps[0][:])
            P.copy(g[:, 512:1024], E.ps[1][:], eng="act")
            P.tt(kk[:], k[:], kkv[:, 0, :], ALU.mult)
            P.tt(t1[:], kk[:], kk[:], ALU.mult)
            P.red(sm[:, 0, :], t1[:].rr("p (h d) -> p h d", h=16))
            P.act(sm[:, 1, :], sm[:, 0, :], AF.Sqrt)
            P.ts(sm[:, 1, :], sm[:, 1, :], 1e-12, 0.0, ALU.max, ALU.add)
            P.recip(sm[:, 2, :], sm[:, 1, :])
            P.tt(kk[:].rr("p (h d) -> p h d", h=16), kk[:].rr("p (h d) -> p h d", h=16),
                 sm[:, 2, :].us(2).bc([128, 16, 64]), ALU.mult)
            P.ts(t1[:], kk[:], -1.0, 0.0, ALU.mult, ALU.add)
            for e in range(2):
                emit_q(e, t1, 0, ti)
                emit_q(e, v, 320, ti)
            xmw = mix(1)
            xma = mix(4)
            for e in range(2):
                lw = lora(xmw, lambda kc: w1[:, e, kc, :], 64, AF.Tanh)
                for hf in range(2):
                    cs = slice(hf * 512, (hf + 1) * 512)
                    P.mm(E.ps[hf][:], lw[0:64, :], w2[:, e, cs])
                    P.tt(we[:, cs], E.ps[hf][:], w0[:, e, cs], ALU.add)
                P.act(we[:], we[:], AF.Exp, scale=-1.0)
                P.act(we[:], we[:], AF.Ln, bias=1.0)
                P.act(we[:], we[:], AF.Exp, scale=-1.0, bias=-0.5)
                P.act(we[:], we[:], AF.Exp, scale=-1.0)
                la = lora(xma, lambda kc: a1[:, e, kc, :], 64, None)
                for hf in range(2):
                    cs = slice(hf * 512, (hf + 1) * 512)
                    P.mm(E.ps[hf][:], la[0:64, :], a2[:, e, cs])
                    P.tt(ae[:, cs], E.ps[hf][:], a0[:, e, cs], ALU.add)
                P.act(ae[:], ae[:], AF.Sigmoid)
                P.stt(kd[:], ae[:], -1.0, kkv[:, 1, :], ALU.add, ALU.mult)
                P.stt(kd[:], kd[:], 1.0, k[:], ALU.add, ALU.mult)
                if e == 0:
                    P.copy(kds[:], kd[:])
                else:
                    P.tt(kds[:], kds[:], kd[:], ALU.add)
                emit_q(e, we, 128, ti)
                emit_q(e, kd, 256, ti)
                P.tt(t1[:], we[:], r[:], ALU.mult)
                emit_q(e, t1, 64, ti)
                P.tt(t2[:], kk[:], ae[:], ALU.mult)
                emit_q(e, t2, 192, ti)
                P.tt(t2[:], t2[:], r[:], ALU.mult)
                P.red(sm[:, 3, :], t2[:].rr("p (h d) -> p h d", h=16))
                emit_s(e, sm[:, 3, :], 384, ti)
                P.tt(t2[:], kd[:], r[:], ALU.mult)
                P.red(sm[:, 4, :], t2[:].rr("p (h d) -> p h d", h=16))
                emit_s(e, sm[:, 4, :], 385, ti)
            P.tt(t1[:], r[:], kkv[:, 2, :], ALU.mult)
            P.tt(t1[:], t1[:], kds[:], ALU.mult)
            P.red(sm[:, 5, :], t1[:].rr("p (h d) -> p h d", h=16))
            P.dma(rows(S["RK"], ti), sm[:, 5, :], q="sp")
            P.dma(rows(S["VV"], ti), v[:], q="sp")
            P.dma(rows(S["GG"], ti), g[:], q="sp")


def st_scan(P, E, S, NT, TC=32):
    NTOK = NT * 128
    with Scope(P):
        St = Tile(P, "S", [128, 16, 64])
        Pt = Tile(P, "Pt", [128, 16, 64])
        inr = Ring(P, "scin", [128, TC, REC], 2)
        tmp = Tile(P, "stmp", [128, 16, 2, 64])
        tmp2 = Tile(P, "stmp2", [128, 16, 64])
        vkr = Ring(P, "vk", [128, 16, 64], 4)
        say = Ring(P, "say", [128, TC, 16, 2], 2)
        vs = Ring(P, "vs", [128, TC, 16], 2)
        yr = Ring(P, "yy", [128, TC, 16], 2)
        y2 = Tile(P, "yy2", [128, TC, 16])
        P.memset(St[:], 0.0)
        for c in range(NTOK // TC):
            IN = inr.next()
            for e in range(2):
                base = S["SC"][e]
                src = bass.AP(base.tensor, base.offset + c * TC * REC, [[NTOK * REC, 16], [0, 4], [1, TC * REC]])
                P.dma(IN[e * 64:(e + 1) * 64].rr("p t r -> p (t r)"), DV(src))
            VS = vs.next()
            P.ts(VS[:], IN[:, :, 320:336], E.mask4[:, 0:1], 0.0, ALU.mult, ALU.add)
            for vb in range(1, 4):
                P.stt(VS[:], IN[:, :, 320 + vb * 16:336 + vb * 16], E.mask4[:, vb:vb + 1], VS[:],
                      ALU.mult, ALU.add)
            SAY = say.next()
            for t in range(TC):
                vk = vkr.next()
                P.tt(vk[:], VS[:, t, :].us(2).bc([128, 16, 64]), IN[:, t, 256:320].us(1).bc([128, 16, 64]),
                     ALU.mult, eng="pool")
                P.tt(tmp[:], St[:].us(2).bc([128, 16, 2, 64]),
                     IN[:, t, 0:128].rr("p (a k) -> p a k", a=2).us(1).bc([128, 16, 2, 64]), ALU.mult)
                P.tt(Pt[:], St[:], IN[:, t, 128:192].us(1).bc([128, 16, 64]), ALU.mult, eng="pool")
                P.red(SAY[:, t], tmp[:])
                P.tt(Pt[:], Pt[:], vk[:], ALU.add, eng="pool")
                P.tt(tmp2[:], SAY[:, t, :, 0].us(2).bc([128, 16, 64]),
                     IN[:, t, 192:256].us(1).bc([128, 16, 64]), ALU.mult)
                P.tt(St[:], Pt[:], tmp2[:], ALU.add)
            Y = yr.next()
            P.tt(Y[:], SAY[:, :, :, 0], IN[:, :, 384:385].bc([128, TC, 16]), ALU.mult)
            P.tt(Y[:], Y[:], SAY[:, :, :, 1], ALU.add)
            P.tt(y2[:], VS[:], IN[:, :, 385:386].bc([128, TC, 16]), ALU.mult)
            P.tt(Y[:], Y[:], y2[:], ALU.add)
            for e in range(2):
                base = S["YO"][e]
                dst = bass.AP(base.tensor, base.offset + c * TC * 1024, [[16, 64], [1024, TC], [1, 16]])
                P.dma(DV(dst), Y[e * 64:(e + 1) * 64], q="sp")


def st_post_odd(P, E, X, W, j, S, NT, last):
    with Scope(P):
        xt = Tile(P, "xt", [128, 1024])
        y = Tile(P, "y", [128, 16, 64])
        yb = Tile(P, "yb", [128, 1024])
        v = Tile(P, "v", [128, 16, 64])
        g = Tile(P, "g", [128, 1024])
        rk = Tile(P, "rk", [128, 16])
        lnw = Tile(P, "lnw", [128, 2, 1024])
        sm = Tile(P, "sm16", [128, 4, 16])
        oT = Tile(P, "oT", [128, 8, 128], dt=F32R)
        wkr = Ring(P, "wk", [128, 1024], 4, dt=F32R)
        tmp = Tile(P, "tmpy", [128, 512])
        P.dma(lnw[:, 0], DV(W["tm_ln_w"][j].partition_broadcast(128)))
        P.dma(lnw[:, 1], DV(W["tm_ln_b"][j].partition_broadcast(128)))
        for ti in range(NT):
            if last and ti < 2:
                continue
            P.dma(xt[:], rows(X, ti))
            P.dma(y[:].rr("p h d -> p (h d)"), rows(S["YO"][0], ti))
            so = rev_step_off(ti, NT)
            P.dma(yb[:], DV(S["YO"][1][so:so + 128]))
            P.dma(v[:].rr("p h d -> p (h d)"), rows(S["VV"], ti))
            P.dma(g[:], rows(S["GG"], ti))
            P.dma(rk[:], rows(S["RK"], ti))
            for hf in range(2):
                P.mm(E.ps[hf][:], E.jrev[:], yb[:, hf * 512:(hf + 1) * 512])
                P.tt(y[:, hf * 8:(hf + 1) * 8, :], y[:, hf * 8:(hf + 1) * 8, :],
                     E.ps[hf][:].rr("p (h d) -> p h d", h=8), ALU.add)
            P.red(sm[:, 0, :], y[:])
            P.ts(sm[:, 0, :], sm[:, 0, :], 1.0 / 64, 0.0, ALU.mult, ALU.add)
            P.tt(y[:], y[:], sm[:, 0, :].us(2).bc([128, 16, 64]), ALU.subtract)
            yq = yb[:].rr("p (h d) -> p h d", h=16)
            P.tt(yq, y[:], y[:], ALU.mult)
            P.red(sm[:, 1, :], yq)
            P.act(sm[:, 2, :], sm[:, 1, :], AF.Sqrt, bias=64e-5, scale=1.0 / 64)
            P.recip(sm[:, 3, :], sm[:, 2, :])
            P.tt(y[:], y[:], sm[:, 3, :].us(2).bc([128, 16, 64]), ALU.mult)
            yf = y[:].rr("p h d -> p (h d)")
            P.tt(yf, yf, lnw[:, 0, :], ALU.mult)
            P.tt(yf, yf, lnw[:, 1, :], ALU.add)
            P.tt(v[:], v[:], rk[:].us(2).bc([128, 16, 64]), ALU.mult)
            P.tt(y[:], y[:], v[:], ALU.add)
            P.tt(yf, yf, g[:], ALU.mult)
            emit_transpose(P, E, oT[:], yf, 8)
            for kc in range(8):
                wt = wkr.next()
                P.dma(wt[:], DV(W["tm_w_o"][j, kc * 128:(kc + 1) * 128, :]), q="pool")
                for hf in range(2):
                    P.mm(E.ps[hf][:], oT[:, kc, :], wt[:, hf * 512:(hf + 1) * 512],
                         start=(kc == 0), stop=(kc == 7), inc=(hf == 1))
            emit_postadd(P, E, tmp, xt[:], [E.ps[0][:], E.ps[1][:]], typ_of(ti))
            P.dma(rows(X, ti), xt[:], q="sp")


def tile_w_in(w):
    g = w[:, :FH].reshape(8, 128, 22, 128).transpose(2, 1, 0, 3)
    u = w[:, FH:].reshape(8, 128, 22, 128).transpose(2, 1, 0, 3)
    return np.ascontiguousarray(np.concatenate([g, u], axis=-1))


def tile_ev_in(w):
    wp = np.zeros((D, 45 * 128), np.float32)
    wp[:, :w.shape[1]] = w
    return np.ascontiguousarray(wp.reshape(8, 128, 45, 128).transpose(2, 1, 0, 3))


def rope_tables(L):
    t = np.arange(L)
    row = (t // 64).astype(np.float32)
    col = (t % 64).astype(np.float32)
    inv = (np.float32(10000.0) ** (-np.arange(16, dtype=np.float32) / np.float32(16))).astype(np.float32)
    ang = np.stack([row[:, None] * inv, col[:, None] * inv])
    f = np.arange(128)
    a = (f % 64) // 32
    i = f % 16
    A = ang[a, :, i]
    return np.cos(A).astype(np.float32), np.sin(A).astype(np.float32)


def make_consts():
    ident = np.eye(128, dtype=np.float32)
    jrev = np.ascontiguousarray(ident[::-1])
    rot = np.zeros((128, 128), np.float32)
    for fp in range(128):
        if (fp % 32) < 16:
            rot[fp + 16, fp] = -1.0
        else:
            rot[fp - 16, fp] = 1.0
    jj, ii = np.meshgrid(np.arange(128), np.arange(128), indexing="ij")
    tri0 = (jj <= ii).astype(np.float32)
    tri1 = (jj >= ii).astype(np.float32)
    mask4 = np.zeros((128, 4), np.float32)
    mask4[np.arange(128), np.arange(128) % 4] = 1.0
    return dict(ident=ident, jrev=jrev, rot=rot, tri0=tri0, tri1=tri1, mask4=mask4)


DEPTH = 4


def build_and_run(inputs, ncores=4, depth=DEPTH, debug_out=None):
    x = np.asarray(inputs["x"], np.float32)
    B, L, _ = x.shape
    NTOK = CTX + L
    NT = NTOK // 128
    LA = Launch(ncores)
    nc = LA.nc
    f = lambda a: np.asarray(a, np.float32)
    ctx = f(inputs["ctx"])
    c = f(inputs["c"])
    c_ctx = f(inputs["c_ctx"])
    XIN = LA.inp("xin", [np.concatenate([ctx[b % B], x[b % B]], axis=0) for b in range(ncores)])
    C = {k: LA.inp(k, v) for k, v in make_consts().items()}
    cv = []
    for b in range(ncores):
        vv = np.stack([c[b % B], c_ctx], axis=-1)
        cv.append(np.ascontiguousarray(vv.reshape(8, 128, 2).transpose(1, 0, 2)))
    C["cvec"] = LA.inp("cvec", cv)
    W = {}
    cosT, sinT = rope_tables(L)
    W["cosT"] = LA.inp("cosT", cosT)
    W["sinT"] = LA.inp("sinT", sinT)
    for name in ("ada_w", "ada_b", "norm_w", "ffn_w_out", "ev_w_out", "da_lambda", "da_subln", "ssd_d",
                 "ssd_norm", "tm_mu", "tm_w_r", "tm_w_k", "tm_w_v", "tm_w_o", "tm_w0", "tm_w1", "tm_w2",
                 "tm_a0", "tm_a1", "tm_a2", "tm_g1", "tm_g2", "tm_k_k", "tm_k_a", "tm_r_k", "tm_ln_w",
                 "tm_ln_b", "tm_v0", "tm_v1", "tm_v2"):
        W[name] = LA.inp(name, f(inputs[name]))
    wi = f(inputs["ffn_w_in"])
    W["ffn_w_in"] = LA.inp("ffn_w_in", np.stack([np.stack([tile_w_in(wi[l, s]) for s in range(2)])
                                                 for l in range(wi.shape[0])]))
    ev = f(inputs["ev_w_in"])
    W["ev_w_in"] = LA.inp("ev_w_in", np.stack([tile_ev_in(ev[j]) for j in range(ev.shape[0])]))
    cw = f(inputs["ssd_conv_w"])
    cb = f(inputs["ssd_conv_b"])
    conv = np.concatenate([cw, cb[:, None, :]], axis=1)
    W["ssd_conv"] = LA.inp("ssd_conv", np.ascontiguousarray(conv.reshape(-1, 6, 12, 128).transpose(0, 3, 2, 1)))
    W["ssd_dt_bias"] = LA.inp("ssd_dt_bias", f(inputs["ssd_dt_bias"]).reshape(-1, 32))
    W["ssd_a_log"] = LA.inp("ssd_a_log", f(inputs["ssd_a_log"]).reshape(-1, 32))
    OUT = LA.out("out", [L, D])
    S = {}
    X = LA.scratch("X", [NTOK, D])
    for nm, shp in (("QT", [8, 128, NTOK]), ("KT", [8, 128, NTOK]), ("V", [NTOK, D]), ("Z", [NTOK, D]),
                    ("XBCT", [12, 128, NTOK]), ("DT", [NTOK, 32]), ("MIXT", [8, 128, NTOK]),
                    ("BCT", [4, 128, NTOK]), ("XS", [NTOK, D]), ("BTOK", [NTOK, 256]), ("YS", [NTOK, D]),
                    ("HT", [8, 128, NTOK]), ("VF", [NTOK, D]), ("RK", [NTOK, 16]), ("VV", [NTOK, D]),
                    ("GG", [NTOK, D])):
        S[nm] = LA.scratch(nm, shp) if (debug_out is None or nm not in debug_out) else LA.out(nm, shp)
    S["SC"] = [LA.scratch(f"SC{e}", [16, NTOK, REC]) for e in range(2)]
    if debug_out is not None and "YO" in debug_out:
        S["YO"] = [LA.out(f"YO{e}", [NTOK, D]) for e in range(2)]
    else:
        S["YO"] = [LA.scratch(f"YO{e}", [NTOK, D]) for e in range(2)]
    if debug_out is not None and "X" in debug_out:
        XD = LA.out("XD", [NTOK, D])
    with ExitStack() as es:
        P = Prog(nc, es)
        E = Env(P, C)
        for l in range(depth):
            j = l // 2
            last = l == DEPTH - 1
            emit_mods(P, E, W, l, 0, 0.5)
            st_ffn(P, E, XIN if l == 0 else X, X, W["ffn_w_in"][l, 0], W["ffn_w_out"][l, 0], NT)
            emit_mods(P, E, W, l, 1, 1.0)
            if l % 2 == 0:
                st_pre_even(P, E, X, W, j, S, NT)
                st_attn(P, E, W, j, l, S, NT)
                st_ssd(P, E, W, j, S, NT)
                st_post_even(P, E, X, W, j, S, NT)
            else:
                st_pre_odd(P, E, X, S, NT)
                st_prep_odd(P, E, W, j, S, NT, use_vgate=(j > 0))
                st_scan(P, E, S, NT)
                st_post_odd(P, E, X, W, j, S, NT, last)
            emit_mods(P, E, W, l, 2, 0.5)
            st_ffn(P, E, X, X, W["ffn_w_in"][l, 1], W["ffn_w_out"][l, 1], NT,
                   out_lat=(OUT if l == depth - 1 else None))
        if debug_out is not None and "X" in debug_out:
            with Scope(P):
                t = Tile(P, "dbg", [128, 1024])
                for ti in range(NT):
                    P.dma(t[:], rows(X, ti))
                    P.dma(rows(XD, ti), t[:])
        P.barrier()
    print("instructions:", P.nins, flush=True)
    res = LA.run()
    return res


def kernel(**inputs):
    x = np.asarray(inputs["x"])
    B = x.shape[0]
    res = build_and_run(inputs, ncores=B)
    return np.stack([np.asarray(res[b]["out"], np.float32) for b in range(B)], axis=0)
```

```python
import math
import numpy as np
from contextlib import ExitStack
import concourse.bass as bass
import concourse.mybir as mybir
from concourse.bass_utils import run_bass_kernel_spmd

F32 = mybir.dt.float32
F32R = mybir.dt.float32r
AF = mybir.ActivationFunctionType
ALU = mybir.AluOpType
AX = mybir.AxisListType
NDS = 24

D = 1024
FH = 2816
NMOD = 9
CTX = 256
NCORES = 8


class Buf:
    __slots__ = ("w", "r")

    def __init__(self):
        self.w = None
        self.r = {}


class View:
    __slots__ = ("ap", "bufs")

    def __init__(self, ap, bufs):
        self.ap = ap
        self.bufs = bufs

    def __getitem__(self, k):
        return View(self.ap[k], self.bufs)

    def rr(self, pat, **kw):
        return View(self.ap.rearrange(pat, **kw), self.bufs)

    def bc(self, shape):
        return View(self.ap.to_broadcast(list(shape)), self.bufs)

    def us(self, ax):
        return View(self.ap.unsqueeze(ax), self.bufs)


_TILE_ID = [0]


class Tile:
    def __init__(self, P, name, shape, dt=F32, psum=False, nb=1):
        _TILE_ID[0] += 1
        name = f"{name}_{_TILE_ID[0]}"
        if psum:
            self.t = P.es.enter_context(P.nc.psum_tensor("t_" + name, list(shape), dt))
        else:
            self.t = P.es.enter_context(P.nc.sbuf_tensor("t_" + name, list(shape), dt))
        self.bufs = [Buf() for _ in range(nb)]

    def __getitem__(self, k):
        return View(self.t[k], self.bufs)

    def part(self, i):
        return View(self.t[:], [self.bufs[i]])


def _bufs(*vs):
    out = []
    for v in vs:
        if isinstance(v, View):
            out.extend(v.bufs)
    return out


def _a(v):
    return v.ap if isinstance(v, View) else v


class Prog:
    def __init__(self, nc, es):
        self.nc = nc
        self.es = es
        self.e = {"pe": nc.tensor, "dve": nc.vector, "act": nc.scalar,
                  "pool": nc.gpsimd, "sp": nc.sync}
        self.sem = {k: es.enter_context(nc.semaphore("s_" + k))
                    for k in ("pe", "dve", "act", "pool")}
        self.cnt = {k: 0 for k in self.sem}
        self.seen = {k: {} for k in self.e}
        self.dsem = [es.enter_context(nc.semaphore(f"d{i}")) for i in range(NDS)]
        self.dcnt = [0] * NDS
        self.dnext = 0
        self.nins = 0
        self.dq = 0

    def _semobj(self, key):
        return self.dsem[key[1]] if isinstance(key, tuple) else self.sem[key]

    def _wait(self, eng, deps):
        best = {}
        for d in deps:
            if d is None:
                continue
            k, v = d
            if eng == "pe" and k == "pe":
                continue
            if best.get(k, 0) < v:
                best[k] = v
        seen = self.seen[eng]
        for k, v in best.items():
            if seen.get(k, 0) >= v:
                continue
            if not isinstance(k, tuple) and v > self.cnt[k]:
                raise RuntimeError(f"wait on un-issued inc {k} {v} > {self.cnt[k]}")
            self.e[eng].wait_ge(self._semobj(k), v)
            seen[k] = v

    def _deps(self, reads, writes):
        deps = []
        for b in reads:
            deps.append(b.w)
        for b in writes:
            deps.append(b.w)
            deps.extend(b.r.items())
        return deps

    def _mark(self, tok, reads, writes):
        k, v = tok
        for b in reads:
            if b.r.get(k, 0) < v:
                b.r[k] = v
        for b in writes:
            b.w = tok
            b.r = {}

    def op(self, eng, reads, writes, fn, inc=True):
        self._wait(eng, self._deps(reads, writes))
        ins = fn(self.e[eng])
        self.nins += 1
        if inc:
            self.cnt[eng] += 1
            ins.then_inc(self.sem[eng], 1)
            tok = (eng, self.cnt[eng])
        else:
            tok = (eng, self.cnt[eng] + 1)
        self._mark(tok, reads, writes)
        return ins

    def dma(self, out, in_, q=None, **kw):
        if q is None:
            q = "sp"
        if _a(out).dtype != _a(in_).dtype:
            q = "pool"
        reads, writes = _bufs(in_), _bufs(out)
        i = self.dnext
        self.dnext = (self.dnext + 1) % NDS
        deps = self._deps(reads, writes)
        if self.dcnt[i]:
            deps.append((("d", i), self.dcnt[i]))
        self._wait(q, deps)
        ins = self.e[q].dma_start(out=_a(out), in_=_a(in_), **kw)
        self.dcnt[i] += 16
        ins.then_inc(self.dsem[i], 16)
        self._mark((("d", i), self.dcnt[i]), reads, writes)
        self.nins += 1
        return ins

    def barrier(self):
        deps = [(k, self.cnt[k]) for k in self.sem if self.cnt[k]]
        deps += [(("d", i), self.dcnt[i]) for i in range(NDS) if self.dcnt[i]]
        for eng in ("pe", "dve", "act", "pool", "sp"):
            self._wait(eng, deps)

    def tt(self, out, a, b, op, eng="dve"):
        return self.op(eng, _bufs(a, b), _bufs(out),
                       lambda e: e.tensor_tensor(_a(out), _a(a), _a(b), op))

    def ts(self, out, a, s1, s2, op0, op1, eng="dve"):
        return self.op(eng, _bufs(a, s1, s2), _bufs(out),
                       lambda e: e.tensor_scalar(_a(out), _a(a), _a(s1), _a(s2), op0, op1))

    def stt(self, out, a, s, b, op0, op1, eng="dve"):
        return self.op(eng, _bufs(a, s, b), _bufs(out),
                       lambda e: e.scalar_tensor_tensor(_a(out), _a(a), _a(s), _a(b), op0, op1))

    def act(self, out, a, func, bias=0.0, scale=1.0, accum=None):
        if accum is None:
            return self.op("act", _bufs(a, bias, scale), _bufs(out),
                           lambda e: e.activation(_a(out), _a(a), func, bias=_a(bias), scale=_a(scale)))
        return self.op("act", _bufs(a, bias, scale), _bufs(out, accum),
                       lambda e: e.activation(_a(out), _a(a), func, bias=_a(bias), scale=_a(scale),
                                              accum_out=_a(accum)))

    def copy(self, out, a, eng="dve"):
        if eng == "act":
            return self.op("act", _bufs(a), _bufs(out), lambda e: e.copy(_a(out), _a(a)))
        return self.op(eng, _bufs(a), _bufs(out), lambda e: e.tensor_copy(_a(out), _a(a)))

    def red(self, out, a, op=ALU.add, axis=AX.X, eng="dve"):
        return self.op(eng, _bufs(a), _bufs(out),
                       lambda e: e.tensor_reduce(_a(out), _a(a), axis, op))

    def recip(self, out, a):
        return self.op("dve", _bufs(a), _bufs(out), lambda e: e.reciprocal(_a(out), _a(a)))

    def memset(self, out, val, eng="dve"):
        return self.op(eng, [], _bufs(out), lambda e: e.memset(_a(out), val))

    def mm(self, out, lhsT, rhs, start=True, stop=True, inc=None):
        return self.op("pe", _bufs(lhsT, rhs), _bufs(out),
                       lambda e: e.matmul(_a(out), _a(lhsT), _a(rhs), start=start, stop=stop),
                       inc=(stop if inc is None else (inc or stop)))

    def tr(self, out, a, ident):
        return self.op("pe", _bufs(a, ident), _bufs(out),
                       lambda e: e.transpose(_a(out), _a(a), _a(ident)))


class Ring:
    def __init__(self, P, name, shape, n, dt=F32, nb=1):
        self.tiles = [Tile(P, f"{name}{i}", shape, dt, nb=nb) for i in range(n)]
        self.i = 0

    def next(self):
        t = self.tiles[self.i]
        self.i = (self.i + 1) % len(self.tiles)
        return t


def DV(ap):
    return View(ap, [])


class Launch:
    def __init__(self, ncores):
        self.nc = bass.Bass("TRN2", target_bir_lowering=False)
        self.ncores = ncores
        self.in_maps = [{} for _ in range(ncores)]

    def inp(self, name, arrs):
        if not isinstance(arrs, list):
            arrs = [arrs] * self.ncores
        for i in range(self.ncores):
            self.in_maps[i][name] = np.ascontiguousarray(arrs[i], dtype=np.float32)
        return self.nc.dram_tensor(name, list(arrs[0].shape), F32, kind="ExternalInput").ap()

    def out(self, name, shape):
        return self.nc.dram_tensor(name, list(shape), F32, kind="ExternalOutput").ap()

    def scratch(self, name, shape):
        return self.nc.dram_tensor(name, list(shape), F32).ap()

    def run(self):
        res = run_bass_kernel_spmd(self.nc, self.in_maps, core_ids=list(range(self.ncores)))
        return res.results


class Env:
    def __init__(self, P, C):
        self.P = P
        self.ps = [Tile(P, f"ps{i}", [128, 512], psum=True) for i in range(8)]
        self.ident = Tile(P, "ident", [128, 128])
        self.jrev = Tile(P, "jrev", [128, 128])
        self.rot = Tile(P, "rot", [128, 128])
        self.tri = [Tile(P, f"tri{e}", [128, 128]) for e in range(2)]
        self.mask4 = Tile(P, "mask4", [128, 4])
        self.ones = Tile(P, "ones", [128, 128])
        self.st = Tile(P, "st", [128, 16])
        self.junk = Tile(P, "junk", [128, 1024])
        self.cs = Tile(P, "cs", [128, 8, 2])
        self.csb = Tile(P, "csb", [128, 16, 128])
        self.mods = [[Tile(P, f"mod{v}_{t}", [128, 1024]) for t in range(2)] for v in range(3)]
        P.dma(self.ident[:], DV(C["ident"]))
        P.dma(self.jrev[:], DV(C["jrev"]))
        P.dma(self.rot[:], DV(C["rot"]))
        P.dma(self.tri[0][:], DV(C["tri0"]))
        P.dma(self.tri[1][:], DV(C["tri1"]))
        P.dma(self.mask4[:], DV(C["mask4"]))
        P.memset(self.ones[:], 1.0)
        self.onesr = Tile(P, "onesr", [128, 128], dt=F32R)
        P.copy(self.onesr[:], self.ones[:])
        P.dma(self.cs[:], DV(C["cvec"]))
        P.act(self.cs[:], self.cs[:], AF.Silu)
        P.copy(self.csb[:], self.cs[:].rr("p k t -> p (k t)").us(2).bc([128, 16, 128]))


class Scope:
    def __init__(self, P):
        self.P = P

    def __enter__(self):
        self.P.barrier()
        self.old = self.P.es
        self.stack = ExitStack()
        self.stack.__enter__()
        self.P.es = self.stack
        return self

    def __exit__(self, *a):
        self.P.barrier()
        self.P.es = self.old
        return self.stack.__exit__(*a)


def emit_mods(P, E, W, l, s, weight):
    with Scope(P):
        nw = Tile(P, "nw", [128, 2, 1024])
        adab = Tile(P, "adab", [1, 1024])
        adaw = Ring(P, "adaw", [128, 8, 128], 3)
        P.dma(nw[:], DV(W["norm_w"][l, 2 * s:2 * s + 2, :].partition_broadcast(128)))
        for v in range(3):
            m = 3 * s + v
            P.dma(adab[:], DV(W["ada_b"][l:l + 1, m * D:(m + 1) * D]))
            for q in range(8):
                wt = adaw.next()
                c0 = m * D + q * 128
                P.dma(wt[:], DV(W["ada_w"][l, :, c0:c0 + 128].rearrange("(kc p) n -> p kc n", p=128)))
                cs = slice(q * 128, (q + 1) * 128)
                for t in range(2):
                    ps = E.ps[(2 * q + t) % 8][:, 0:128]
                    for kc in range(8):
                        P.mm(ps, E.csb[:, kc * 2 + t, :], wt[:, kc, :], start=(kc == 0), stop=False)
                    P.mm(ps, E.ones[0:1, :], adab[0:1, cs], start=False, stop=True)
                    if v == 0:
                        P.copy(E.mods[1][t][:, cs], ps)
                    elif v == 1:
                        P.stt(E.mods[0][t][:, cs], ps, 1.0, nw[:, 0, cs], ALU.add, ALU.mult)
                    else:
                        P.stt(E.mods[2][t][:, cs], ps, float(weight), nw[:, 1, cs], ALU.mult, ALU.mult)


def emit_transpose(P, E, dst, src, ncol, npart=128):
    for c0 in range(0, ncol, 4):
        n = min(4, ncol - c0)
        ps = E.ps[6 + (c0 // 4) % 2]
        for c in range(n):
            P.tr(ps[:, c * 128:c * 128 + npart], src[0:npart, (c0 + c) * 128:(c0 + c + 1) * 128],
                 E.ident[0:npart, 0:npart])
        P.copy(dst[:, c0:c0 + n, :], ps[:, 0:n * 128].rr("p (c t) -> p c t", c=n)[:, :, 0:npart])


def emit_prenorm(P, E, h, x, typ):
    st = E.st
    P.act(E.junk[:, 0:1024], x, AF.Square, accum=st[:, 0:1])
    P.act(st[:, 1:2], st[:, 0:1], AF.Sqrt, bias=1e-6, scale=1.0 / 1024)
    P.recip(st[:, 2:3], st[:, 1:2])
    P.stt(h, x, st[:, 2:3], E.mods[0][typ][:], ALU.mult, ALU.mult)
    P.tt(h, h, E.mods[1][typ][:], ALU.add)


def emit_postadd(P, E, tmp, xv, ys, typ):
    st = E.st
    for hf in range(2):
        P.act(E.junk[:, 0:512], ys[hf], AF.Square, accum=st[:, 4 + hf:5 + hf])
    P.tt(st[:, 6:7], st[:, 4:5], st[:, 5:6], ALU.add)
    P.act(st[:, 7:8], st[:, 6:7], AF.Sqrt, bias=1e-6, scale=1.0 / 1024)
    P.recip(st[:, 8:9], st[:, 7:8])
    for hf in range(2):
        cs = slice(hf * 512, (hf + 1) * 512)
        P.stt(tmp[:], ys[hf], st[:, 8:9], E.mods[2][typ][:, cs], ALU.mult, ALU.mult)
        P.tt(xv[:, cs], xv[:, cs], tmp[:], ALU.add)


def groups_of(NT):
    g = [[0, 1]]
    for t0 in range(2, NT, 4):
        g.append(list(range(t0, min(t0 + 4, NT))))
    return g


def typ_of(ti):
    return 1 if ti < 2 else 0


def rows(ap, ti, n=128):
    return DV(ap[ti * 128:ti * 128 + n])


def st_ffn(P, E, x_in, x_out, w_in, w_out, NT, out_lat=None):
    with Scope(P):
        xg = Tile(P, "xg", [128, 4, 1024], nb=4)
        h = Tile(P, "hh", [128, 1024])
        hT = Tile(P, "hT", [128, 8, 512], dt=F32R)
        actT = Tile(P, "actT", [128, 22, 512], dt=F32R)
        sgr = Ring(P, "sg", [128, 512], 2)
        wir = Ring(P, "wi", [128, 8, 256], 2, dt=F32R)
        wor = Ring(P, "wo", [128, 1024], 2, dt=F32R)
        tmp = Tile(P, "tmpy", [128, 512])
        for tl in groups_of(NT):
            if out_lat is not None and tl[0] < 2:
                continue
            T = len(tl) * 128
            for i, ti in enumerate(tl):
                xv = xg.part(i)[:, i, :]
                P.dma(xv, rows(x_in, ti))
                emit_prenorm(P, E, h[:], xv, typ_of(ti))
                emit_transpose(P, E, hT[:, :, i * 128:(i + 1) * 128], h[:], 8)
            for j in range(22):
                wi = wir.next()
                P.dma(wi[:], DV(w_in[j]), q="pool")
                pg = E.ps[j % 2]
                pu = E.ps[2 + j % 2]
                for kc in range(8):
                    P.mm(pg[:, 0:T], wi[:, kc, 0:128], hT[:, kc, 0:T], start=(kc == 0), stop=(kc == 7))
                for kc in range(8):
                    P.mm(pu[:, 0:T], wi[:, kc, 128:256], hT[:, kc, 0:T], start=(kc == 0), stop=(kc == 7))
                sg = sgr.next()
                P.act(sg[:, 0:T], pg[:, 0:T], AF.Silu)
                P.tt(actT[:, j, 0:T], sg[:, 0:T], pu[:, 0:T], ALU.mult)
            for j in range(22):
                wo = wor.next()
                P.dma(wo[:], DV(w_out[j * 128:(j + 1) * 128, :]), q="pool")
                for i in range(len(tl)):
                    for hf in range(2):
                        P.mm(E.ps[i * 2 + hf][:], actT[:, j, i * 128:(i + 1) * 128],
                             wo[:, hf * 512:(hf + 1) * 512], start=(j == 0), stop=(j == 21),
                             inc=(i == len(tl) - 1 and hf == 1))
            for i, ti in enumerate(tl):
                xv = xg.part(i)[:, i, :]
                emit_postadd(P, E, tmp, xv, [E.ps[i * 2][:], E.ps[i * 2 + 1][:]], typ_of(ti))
                if out_lat is not None:
                    P.dma(rows(out_lat, ti - 2), xv, q="sp")
                else:
                    P.dma(rows(x_out, ti), xv, q="sp")


def st_pre_even(P, E, X, W, j, S, NT):
    wb = W["ev_w_in"]
    with Scope(P):
        xt = Tile(P, "xt", [128, 1024])
        h = Tile(P, "hh", [128, 1024])
        hT = Tile(P, "hT", [128, 8, 512], dt=F32R)
        wr = Ring(P, "wblk", [128, 8, 128], 3, dt=F32R)
        w4r = Ring(P, "wblk4", [128, 4, 8, 128], 2, dt=F32R)
        wdt = Tile(P, "wdt", [128, 8, 128], dt=F32R)
        qsr = Ring(P, "qs", [128, 512], 3)
        tmr = Ring(P, "tm", [128, 512], 2)
        cs = Tile(P, "cosg", [128, 512])
        sn = Tile(P, "sing", [128, 512])
        dtb = Tile(P, "dtb", [128, 32])
        dtt = Ring(P, "dtt", [128, 32], 2)
        P.dma(dtb[:], DV(W["ssd_dt_bias"][j:j + 1, :].partition_broadcast(128)).rr("p o n -> p (o n)"))
        P.dma(wdt[:], DV(wb[j, 44]), q="pool")
        for tl in groups_of(NT):
            T = len(tl) * 128
            t0 = tl[0] * 128
            lat = tl[0] >= 2
            for i, ti in enumerate(tl):
                P.dma(xt[:], rows(X, ti))
                emit_prenorm(P, E, h[:], xt[:], typ_of(ti))
                emit_transpose(P, E, hT[:, :, i * 128:(i + 1) * 128], h[:], 8)
            if lat:
                P.dma(cs[:, 0:T], DV(W["cosT"][:, t0 - CTX:t0 - CTX + T]))
                P.dma(sn[:, 0:T], DV(W["sinT"][:, t0 - CTX:t0 - CTX + T]))
            for blk in list(range(16)) + list(range(32, 44)):
                wt = wr.next()
                P.dma(wt[:], DV(wb[j, blk]), q="pool")
                ps = E.ps[blk % 2]
                for kc in range(8):
                    P.mm(ps[:, 0:T], wt[:, kc, :], hT[:, kc, 0:T], start=(kc == 0), stop=(kc == 7))
                qs = qsr.next()
                P.copy(qs[:, 0:T], ps[:, 0:T], eng="act")
                if blk < 16:
                    if lat:
                        ps2 = E.ps[2 + blk % 2]
                        P.mm(ps2[:, 0:T], E.rot[:], qs[:, 0:T])
                        tm = tmr.next()
                        P.tt(tm[:, 0:T], ps2[:, 0:T], sn[:, 0:T], ALU.mult)
                        P.tt(qs[:, 0:T], qs[:, 0:T], cs[:, 0:T], ALU.mult)
                        P.tt(qs[:, 0:T], qs[:, 0:T], tm[:, 0:T], ALU.add)
                    dst = S["QT"] if blk < 8 else S["KT"]
                    P.dma(DV(dst[blk % 8, :, t0:t0 + T]), qs[:, 0:T], q="sp")
                else:
                    P.dma(DV(S["XBCT"][blk - 32, :, t0:t0 + T]), qs[:, 0:T], q="sp")
            for cb in range(4):
                w4 = w4r.next()
                P.dma(w4[:], DV(wb[j, 16 + cb * 4:16 + cb * 4 + 4].rearrange("b p k n -> p b k n")), q="pool")
                for i, ti in enumerate(tl):
                    ps = E.ps[4 + i % 2]
                    for kc in range(8):
                        P.mm(ps[:].rr("p (b n) -> p b n", b=4), hT[:, kc, i * 128:(i + 1) * 128],
                             w4[:, :, kc, :], start=(kc == 0), stop=(kc == 7))
                    qs = qsr.next()
                    P.copy(qs[:], ps[:], eng="act")
                    dst = S["V"] if cb < 2 else S["Z"]
                    P.dma(DV(dst[ti * 128:(ti + 1) * 128, (cb % 2) * 512:(cb % 2 + 1) * 512]), qs[:], q="sp")
            for i, ti in enumerate(tl):
                ps = E.ps[4 + i % 2]
                for kc in range(8):
                    P.mm(ps[:, 0:32], hT[:, kc, i * 128:(i + 1) * 128], wdt[:, kc, 0:32],
                         start=(kc == 0), stop=(kc == 7))
                d = dtt.next()
                P.tt(d[:], ps[:, 0:32], dtb[:], ALU.add)
                P.act(d[:], d[:], AF.Exp)
                P.act(d[:], d[:], AF.Ln, bias=1.0)
                P.dma(rows(S["DT"], ti), d[:], q="sp")


def st_attn(P, E, W, j, l, S, NT):
    lam_init = 0.8 - 0.6 * math.exp(-0.3 * l)
    NTOK = NT * 128
    with Scope(P):
        KTs = Tile(P, "KTs", [128, NTOK], dt=F32R)
        Vs = Tile(P, "Vs", [128, NT, 128], dt=F32R)
        Qr = Ring(P, "Qs", [128, 512], 2, dt=F32R)
        er = Ring(P, "eexp", [128, 512], 4, dt=F32R)
        lamt = Tile(P, "lamt", [128, 4, 64])
        lams = Tile(P, "lams", [128, 8])
        sub = Tile(P, "subw", [128, 2])
        a0 = Tile(P, "a0", [128, 512])
        a1 = Tile(P, "a1", [128, 512])
        rz = Tile(P, "rz", [128, 512])
        zacc = [Tile(P, f"zacc{m}", [128, 512]) for m in range(2)]
        P.dma(lamt[:], DV(W["da_lambda"][j].partition_broadcast(128)))
        P.tt(lamt[:, 0, :], lamt[:, 0, :], lamt[:, 1, :], ALU.mult)
        P.tt(lamt[:, 2, :], lamt[:, 2, :], lamt[:, 3, :], ALU.mult)
        P.red(lams[:, 0:1], lamt[:, 0, :])
        P.red(lams[:, 1:2], lamt[:, 2, :])
        P.act(lams[:, 2:4], lams[:, 0:2], AF.Exp)
        P.tt(lams[:, 4:5], lams[:, 3:4], lams[:, 2:3], ALU.subtract)
        P.ts(lams[:, 5:6], lams[:, 4:5], -lam_init, 1.0, ALU.add, ALU.mult)
        P.dma(sub[:, 0:1], DV(W["da_subln"][j].rearrange("(p o) -> p o", o=1)))
        P.ts(sub[:, 1:2], sub[:, 0:1], 1.0 - lam_init, 0.0, ALU.mult, ALU.add)
        nlam = lams[:, 5:6]
        for hd in range(8):
            P.dma(KTs[:], DV(S["KT"][hd]), q="pool")
            P.dma(Vs[:], DV(S["V"][:, hd * 128:(hd + 1) * 128].rearrange("(t p) e -> p t e", p=128)), q="pool")
            blocks = [(0, 256, 2)] + [(t0, min(512, NTOK - t0), NT) for t0 in range(256, NTOK, 512)]
            for (t0, T, nk) in blocks:
                Qs = Qr.next()
                P.dma(Qs[:, 0:T], DV(S["QT"][hd, :, t0:t0 + T]), q="pool")
                O = [E.ps[4], E.ps[5]]
                Z = [E.ps[6], E.ps[7]]
                tiles = [(kt, m) for kt in range(nk) for m in range(2)]
                LAG = 2
                for idx in range(len(tiles) + LAG):
                    if idx < len(tiles):
                        kt, m = tiles[idx]
                        P.mm(E.ps[idx % 4][:, 0:T], KTs[m * 64:(m + 1) * 64, kt * 128:(kt + 1) * 128],
                             Qs[m * 64:(m + 1) * 64, 0:T])
                    jx = idx - LAG
                    if jx >= 0:
                        kt, m = tiles[jx]
                        ee = er.next()
                        P.act(ee[:, 0:T], E.ps[jx % 4][:, 0:T], AF.Exp, scale=0.125)
                        P.mm(O[m][:, 0:T], Vs[:, kt, :], ee[:, 0:T], start=(kt == 0), stop=(kt == nk - 1), inc=True)
                        ee32 = View(ee[:, 0:T].ap.bitcast(F32), ee.bufs)
                        if kt == 0:
                            P.copy(zacc[m][:, 0:T], ee32)
                        else:
                            P.tt(zacc[m][:, 0:T], zacc[m][:, 0:T], ee32, ALU.add)
                for m in range(2):
                    P.mm(Z[m][:, 0:T], E.ones[:], zacc[m][:, 0:T])
                P.recip(rz[:, 0:T], Z[0][:, 0:T])
                P.tt(a0[:, 0:T], O[0][:, 0:T], rz[:, 0:T], ALU.mult)
                P.recip(rz[:, 0:T], Z[1][:, 0:T])
                P.tt(a1[:, 0:T], O[1][:, 0:T], rz[:, 0:T], ALU.mult)
                P.stt(a0[:, 0:T], a1[:, 0:T], nlam, a0[:, 0:T], ALU.mult, ALU.add)
                P.tt(a1[:, 0:T], a0[:, 0:T], a0[:, 0:T], ALU.mult)
                P.mm(E.ps[0][:, 0:T], E.ones[:], a1[:, 0:T])
                P.act(rz[:, 0:T], E.ps[0][:, 0:T], AF.Sqrt, bias=1e-5, scale=1.0 / 128)
                P.recip(rz[:, 0:T], rz[:, 0:T])
                P.stt(a0[:, 0:T], a0[:, 0:T], sub[:, 1:2], rz[:, 0:T], ALU.mult, ALU.mult)
                P.dma(DV(S["MIXT"][hd, :, t0:t0 + T]), a0[:, 0:T], q="sp")


def st_ssd(P, E, W, j, S, NT):
    NTOK = NT * 128
    with Scope(P):
        cw = Tile(P, "cw", [128, 12, 6])
        xin = Ring(P, "cxin", [128, 132], 3)
        acc = Ring(P, "cacc", [128, 128], 3)
        tk = Tile(P, "tk", [128, 1280])
        P.dma(cw[:], DV(W["ssd_conv"][j]))
        for c in range(NT):
            lz = c in (0, 2)
            rz = c in (1, NT - 1)
            for blk in range(12):
                xi = xin.next()
                lo = 2 if lz else 0
                hi = 130 if rz else 132
                if lz:
                    P.memset(xi[:, 0:2], 0.0)
                if rz:
                    P.memset(xi[:, 130:132], 0.0)
                P.dma(xi[:, lo:hi], DV(S["XBCT"][blk, :, c * 128 - 2 + lo:c * 128 - 2 + hi]))
                a = acc.next()
                P.ts(a[:], xi[:, 0:128], cw[:, blk, 0:1], cw[:, blk, 5:6], ALU.mult, ALU.add)
                for q in range(1, 5):
                    P.stt(a[:], xi[:, q:q + 128], cw[:, blk, q:q + 1], a[:], ALU.mult, ALU.add)
                P.act(a[:], a[:], AF.Silu)
                if blk >= 8:
                    P.dma(DV(S["BCT"][blk - 8, :, c * 128:(c + 1) * 128]), a[:], q="sp")
                if blk < 10:
                    ps = E.ps[blk // 4]
                    P.tr(ps[:, (blk % 4) * 128:(blk % 4 + 1) * 128], a[:], E.ident[:])
            P.copy(tk[:, 0:512], E.ps[0][:])
            P.copy(tk[:, 512:1024], E.ps[1][:], eng="act")
            P.copy(tk[:, 1024:1280], E.ps[2][:, 0:256])
            P.dma(rows(S["XS"], c), tk[:, 0:1024], q="sp")
            P.dma(rows(S["BTOK"], c), tk[:, 1024:1280], q="sp")
    for e in range(2):
        with Scope(P):
            H = Tile(P, "H", [128, 16, 64])
            Abc = Tile(P, "Abc", [128, 16])
            Dbc = Tile(P, "Dbc", [128, 16])
            xs = Tile(P, "xs", [128, 16, 64])
            yf = Tile(P, "yf", [128, 16, 64])
            dt16 = Tile(P, "dt16", [128, 16])
            bct = Tile(P, "bct", [128, 4, 128])
            btok = Tile(P, "btok", [128, 256])
            sm = Tile(P, "sm", [128, 8, 16])
            dtAb = Tile(P, "dtAb", [128, 16, 128])
            xdt = Tile(P, "xdt", [128, 16, 64])
            xw = Tile(P, "xw", [128, 16, 64])
            cbm = [Tile(P, f"cbm{g}", [128, 128]) for g in range(2)]
            argr = Ring(P, "arg", [128, 128], 3)
            mtr = Ring(P, "mt", [128, 128], 3)
            y = Tile(P, "ysb", [128, 16, 64])
            P.memset(H[:], 0.0)
            P.dma(Abc[:], DV(W["ssd_a_log"][j:j + 1, e * 16:(e + 1) * 16].partition_broadcast(128)).rr("p o n -> p (o n)"))
            P.act(Abc[:], Abc[:], AF.Exp)
            P.ts(Abc[:], Abc[:], -1.0, 0.0, ALU.mult, ALU.add)
            P.dma(Dbc[:], DV(W["ssd_d"][j:j + 1, :].partition_broadcast(128)).rr("p o n -> p (o n)"))
            order = list(range(NT)) if e == 0 else [1, 0] + list(range(NT - 1, 1, -1))
            tri = E.tri[e]
            for c in order:
                P.dma(xs[:].rr("p h d -> p (h d)"), rows(S["XS"], c))
                P.dma(dt16[:], DV(S["DT"][c * 128:(c + 1) * 128, e * 16:(e + 1) * 16]))
                P.dma(bct[:], DV(S["BCT"][:, :, c * 128:(c + 1) * 128].rearrange("b p t -> p b t")))
                P.dma(btok[:], rows(S["BTOK"], c))
                if e == 1:
                    P.dma(yf[:].rr("p h d -> p (h d)"), rows(S["YS"], c))
                dtA = sm[:, 0, :]
                P.tt(dtA, dt16[:], Abc[:], ALU.mult)
                P.copy(dtAb[:], dtA.us(2).bc([128, 16, 128]))
                P.mm(E.ps[7][:, 0:16], tri[:], dtA)
                P.mm(E.ps[7][:, 16:32], E.ones[:], dtA)
                P.copy(sm[:, 1:3, :], E.ps[7][:, 0:32].rr("p (a h) -> p a h", a=2))
                P.act(sm[:, 3:5, :], sm[:, 1:3, :], AF.Exp)
                P.tt(sm[:, 5, :], sm[:, 2, :], sm[:, 1, :], ALU.subtract)
                P.act(sm[:, 5, :], sm[:, 5, :], AF.Exp)
                P.tt(sm[:, 6, :], sm[:, 5, :], dt16[:], ALU.mult)
                P.tt(xdt[:], xs[:], dt16[:].us(2).bc([128, 16, 64]), ALU.mult)
                P.tt(xw[:], xs[:], sm[:, 6, :].us(2).bc([128, 16, 64]), ALU.mult)
                for g in range(2):
                    P.mm(E.ps[6][:, g * 128:(g + 1) * 128], bct[:, g, :], bct[:, 2 + g, :])
                    P.tt(cbm[g][:], E.ps[6][:, g * 128:(g + 1) * 128], tri[:], ALU.mult)
                for hd in range(16):
                    g = hd // 8
                    pr = E.ps[hd % 2]
                    P.mm(pr[:, 0:128], dtAb[:, hd, :], tri[:])
                    ar = argr.next()
                    P.ts(ar[:], pr[:, 0:128], sm[:, 1, hd:hd + 1], 0.0, ALU.subtract, ALU.min)
                    P.act(ar[:], ar[:], AF.Exp)
                    mt = mtr.next()
                    P.tt(mt[:], ar[:], cbm[g][:], ALU.mult, eng="pool")
                    P.mm(E.ps[2 + g][:, (hd % 8) * 64:(hd % 8 + 1) * 64], mt[:], xdt[:, hd, :])
                    P.mm(E.ps[4 + g][:, (hd % 8) * 64:(hd % 8 + 1) * 64], bct[:, 2 + g, :], H[:, hd, :])
                for g in range(2):
                    hs = slice(g * 8, (g + 1) * 8)
                    P.tt(y[:, hs, :], E.ps[4 + g][:].rr("p (h d) -> p h d", h=8),
                         sm[:, 3, hs].us(2).bc([128, 8, 64]), ALU.mult)
                    P.tt(y[:, hs, :], y[:, hs, :], E.ps[2 + g][:].rr("p (h d) -> p h d", h=8), ALU.add)
                if e == 1:
                    P.tt(y[:], y[:], yf[:], ALU.add)
                    P.tt(yf[:], xs[:], Dbc[:].us(2).bc([128, 16, 64]), ALU.mult)
                    P.tt(y[:], y[:], yf[:], ALU.add)
                P.dma(rows(S["YS"], c), y[:].rr("p h d -> p (h d)"), q="sp")
                for g in range(2):
                    P.mm(E.ps[6 + g][:], btok[:, g * 128:(g + 1) * 128],
                         xw[:, g * 8:(g + 1) * 8, :].rr("p h d -> p (h d)"))
                P.tt(H[:], H[:], sm[:, 4, :].us(2).bc([128, 16, 64]), ALU.mult)
                for g in range(2):
                    hs = slice(g * 8, (g + 1) * 8)
                    P.tt(H[:, hs, :], H[:, hs, :], E.ps[6 + g][:].rr("p (h d) -> p h d", h=8), ALU.add)


def st_post_even(P, E, X, W, j, S, NT):
    with Scope(P):
        xg = Tile(P, "xg", [128, 4, 1024], nb=4)
        ys = Tile(P, "ysb", [128, 1024])
        zs = Tile(P, "zsb", [128, 1024])
        nrm = Tile(P, "nrmbc", [128, 1024])
        fT = Tile(P, "fT", [128, 16, 512], dt=F32R)
        wor = Ring(P, "wo", [128, 1024], 3, dt=F32R)
        tmp = Tile(P, "tmpy", [128, 512])
        P.dma(nrm[:], DV(W["ssd_norm"][j:j + 1, :].partition_broadcast(128)).rr("p o n -> p (o n)"))
        st = E.st
        for tl in groups_of(NT):
            T = len(tl) * 128
            t0 = tl[0] * 128
            for i, ti in enumerate(tl):
                P.dma(xg.part(i)[:, i, :], rows(X, ti))
                P.dma(ys[:], rows(S["YS"], ti))
                P.dma(zs[:], rows(S["Z"], ti))
                P.act(zs[:], zs[:], AF.Silu)
                P.tt(ys[:], ys[:], zs[:], ALU.mult)
                for gg in range(2):
                    cs = slice(gg * 512, (gg + 1) * 512)
                    P.act(E.junk[:, 0:512], ys[:, cs], AF.Square, accum=st[:, 10:11])
                    P.act(st[:, 11:12], st[:, 10:11], AF.Sqrt, bias=1e-5, scale=1.0 / 512)
                    P.recip(st[:, 12:13], st[:, 11:12])
                    P.stt(ys[:, cs], ys[:, cs], st[:, 12:13], nrm[:, cs], ALU.mult, ALU.mult)
                emit_transpose(P, E, fT[:, 8:16, i * 128:(i + 1) * 128], ys[:], 8)
            P.dma(fT[:, 0:8, 0:T], DV(S["MIXT"][:, :, t0:t0 + T].rearrange("b p t -> p b t")), q="pool")
            for blk in range(16):
                wo = wor.next()
                P.dma(wo[:], DV(W["ev_w_out"][j, blk * 128:(blk + 1) * 128, :]), q="pool")
                for i in range(len(tl)):
                    for hf in range(2):
                        P.mm(E.ps[i * 2 + hf][:], fT[:, blk, i * 128:(i + 1) * 128],
                             wo[:, hf * 512:(hf + 1) * 512], start=(blk == 0), stop=(blk == 15),
                             inc=(i == len(tl) - 1 and hf == 1))
            for i, ti in enumerate(tl):
                xv = xg.part(i)[:, i, :]
                emit_postadd(P, E, tmp, xv, [E.ps[i * 2][:], E.ps[i * 2 + 1][:]], typ_of(ti))
                P.dma(rows(X, ti), xv, q="sp")


REC = 386


def st_pre_odd(P, E, X, S, NT):
    with Scope(P):
        xt = Tile(P, "xt", [128, 1024])
        h = Tile(P, "hh", [128, 1024])
        hT = Ring(P, "hT1", [128, 8, 128], 2)
        for ti in range(NT):
            P.dma(xt[:], rows(X, ti))
            emit_prenorm(P, E, h[:], xt[:], typ_of(ti))
            t = hT.next()
            emit_transpose(P, E, t[:], h[:], 8)
            P.dma(DV(S["HT"][:, :, ti * 128:(ti + 1) * 128].rearrange("k p t -> p k t")), t[:], q="sp")


def rev_step_off(ti, NT):
    if ti < 2:
        return (1 - ti) * 128
    return CTX + (NT * 128 - (ti + 1) * 128)


def st_prep_odd(P, E, W, j, S, NT, use_vgate):
    NTOK = NT * 128
    with Scope(P):
        hx = Tile(P, "hx", [128, 8, 130])
        xx = Tile(P, "xx", [128, 8, 128])
        xmr = Ring(P, "xm", [128, 8, 128], 2, dt=F32R)
        mu = Tile(P, "mu", [128, 6, 8])
        wkr = Ring(P, "wk", [128, 1024], 4, dt=F32R)
        w1 = Tile(P, "w1", [128, 2, 8, 64], dt=F32R)
        w2 = Tile(P, "w2", [64, 2, 1024], dt=F32R)
        a1 = Tile(P, "a1", [128, 2, 8, 64], dt=F32R)
        a2 = Tile(P, "a2", [64, 2, 1024], dt=F32R)
        g1 = Tile(P, "g1", [128, 8, 160], dt=F32R)
        g2a = Tile(P, "g2a", [128, 1024], dt=F32R)
        g2b = Tile(P, "g2b", [32, 1024], dt=F32R)
        w0 = Tile(P, "w0", [128, 2, 1024])
        a0 = Tile(P, "a0", [128, 2, 1024])
        kkv = Tile(P, "kkv", [128, 3, 1024])
        lo = Ring(P, "lo", [128, 128], 3, dt=F32R)
        r = Tile(P, "r", [128, 1024])
        k = Tile(P, "k", [128, 1024])
        v = Tile(P, "v", [128, 1024])
        g = Tile(P, "g", [128, 1024])
        kk = Tile(P, "kk", [128, 1024])
        we = Tile(P, "we", [128, 1024])
        ae = Tile(P, "ae", [128, 1024])
        kd = Tile(P, "kd", [128, 1024])
        kds = Tile(P, "kds", [128, 1024])
        t1 = Tile(P, "t1", [128, 1024])
        t2 = Tile(P, "t2", [128, 1024])
        sm = Tile(P, "sm16", [128, 8, 16])
        rvr = Ring(P, "rv", [128, 1024], 2)
        if use_vgate:
            v1 = Tile(P, "v1", [128, 8, 32], dt=F32R)
            v2 = Tile(P, "v2", [32, 1024], dt=F32R)
            v0 = Tile(P, "v0", [128, 1024])
            vf = Tile(P, "vf", [128, 1024])
            P.dma(v1[:], DV(W["tm_v1"][0].rearrange("(kc p) r -> p kc r", p=128)), q="pool")
            P.dma(v2[:], DV(W["tm_v2"][0]))
            P.dma(v0[:], DV(W["tm_v0"][0].partition_broadcast(128)))
        P.dma(mu[:], DV(W["tm_mu"][j].rearrange("i (kc p) -> p i kc", p=128)), allow_slow_non_contiguous=True)
        for e in range(2):
            P.dma(w1[:, e], DV(W["tm_w1"][j, e].rearrange("(kc p) r -> p kc r", p=128)), q="pool")
            P.dma(a1[:, e], DV(W["tm_a1"][j, e].rearrange("(kc p) r -> p kc r", p=128)), q="pool")
            P.dma(w2[:, e], DV(W["tm_w2"][j, e]))
            P.dma(a2[:, e], DV(W["tm_a2"][j, e]))
        P.dma(g1[:], DV(W["tm_g1"][j].rearrange("(kc p) r -> p kc r", p=128)), q="pool")
        P.dma(g2a[:], DV(W["tm_g2"][j, 0:128, :]))
        P.dma(g2b[:], DV(W["tm_g2"][j, 128:160, :]))
        P.dma(w0[:], DV(W["tm_w0"][j].partition_broadcast(128)))
        P.dma(a0[:], DV(W["tm_a0"][j].partition_broadcast(128)))
        P.dma(kkv[:, 0], DV(W["tm_k_k"][j].partition_broadcast(128)))
        P.dma(kkv[:, 1], DV(W["tm_k_a"][j].partition_broadcast(128)))
        P.dma(kkv[:, 2], DV(W["tm_r_k"][j].partition_broadcast(128)))

        def mix(i):
            xm = xmr.next()
            P.tt(xm[:], xx[:], mu[:, i, :].us(2).bc([128, 8, 128]), ALU.mult)
            P.tt(xm[:], xm[:], hx[:, :, 1:129], ALU.add)
            return xm

        def proj(xm, wname, dst):
            for kc in range(8):
                wt = wkr.next()
                P.dma(wt[:], DV(W[wname][j, kc * 128:(kc + 1) * 128, :]), q="pool")
                for hf in range(2):
                    P.mm(E.ps[hf][:], xm[:, kc, :], wt[:, hf * 512:(hf + 1) * 512],
                         start=(kc == 0), stop=(kc == 7), inc=(hf == 1))
            P.copy(dst[:, 0:512], E.ps[0][:])
            P.copy(dst[:, 512:1024], E.ps[1][:], eng="act")

        def lora(xm, w1v, R, func):
            ps = E.ps[2]
            for kc in range(8):
                P.mm(ps[0:R, 0:128], w1v(kc), xm[:, kc, :], start=(kc == 0), stop=(kc == 7))
            t = lo.next()
            if func is None:
                P.copy(t[0:R, :], ps[0:R, 0:128])
            else:
                P.act(t[0:R, :], ps[0:R, 0:128], func)
            return t

        def write_rec(e, src3, q0, n, ti):
            if e == 0:
                so = ti * 128
            else:
                so = rev_step_off(ti, NT)
            base = S["SC"][e]
            dst = bass.AP(base.tensor, base.offset + so * REC + q0, [[REC, 128], [NTOK * REC, 16], [1, n]])
            for h0 in range(0, 16, 4):
                d4 = bass.AP(base.tensor, base.offset + so * REC + q0 + h0 * NTOK * REC,
                             [[REC, 128], [NTOK * REC, 4], [1, n]])
                if n == 1:
                    P.dma(DV(d4), src3[:, h0:h0 + 4, :], q="sp", allow_slow_non_contiguous=True)
                else:
                    P.dma(DV(d4), src3[:, h0:h0 + 4, :], q="sp")

        def emit_q(e, src, q0, ti):
            if e == 0:
                write_rec(0, src[:].rr("p (h d) -> p h d", h=16), q0, 64, ti)
            else:
                rv = rvr.next()
                for hf in range(2):
                    P.mm(E.ps[4 + hf][:], E.jrev[:], src[:, hf * 512:(hf + 1) * 512])
                P.copy(rv[:, 0:512], E.ps[4][:])
                P.copy(rv[:, 512:1024], E.ps[5][:], eng="act")
                write_rec(1, rv[:].rr("p (h d) -> p h d", h=16), q0, 64, ti)

        def emit_s(e, src16, q0, ti):
            if e == 0:
                write_rec(0, src16.us(2), q0, 1, ti)
            else:
                rv = rvr.next()
                P.mm(E.ps[4][:, 0:16], E.jrev[:], src16)
                P.copy(rv[:, 0:16], E.ps[4][:, 0:16])
                write_rec(1, rv[:, 0:16].us(2), q0, 1, ti)

        for ti in range(NT):
            t0 = ti * 128
            lz = ti in (0, 2)
            rz = ti in (1, NT - 1)
            lo_c = 1 if lz else 0
            hi_c = 129 if rz else 130
            if lz:
                P.memset(hx[:, :, 0:1], 0.0)
            if rz:
                P.memset(hx[:, :, 129:130], 0.0)
            for kc in range(8):
                P.dma(hx[:, kc, lo_c:hi_c], DV(S["HT"][kc, :, t0 - 1 + lo_c:t0 - 1 + hi_c]))
            P.tt(xx[:], hx[:, :, 0:128], hx[:, :, 2:130], ALU.add)
            P.stt(xx[:], xx[:], 0.5, hx[:, :, 1:129], ALU.mult, ALU.subtract)
            proj(mix(0), "tm_w_r", r)
            proj(mix(2), "tm_w_k", k)
            xmv = mix(3)
            proj(xmv, "tm_w_v", v)
            if use_vgate:
                vl = lora(xmv, lambda kc: v1[:, kc, :], 32, None)
                for hf in range(2):
                    P.mm(E.ps[hf][:], vl[0:32, :], v2[:, hf * 512:(hf + 1) * 512])
                    cs = slice(hf * 512, (hf + 1) * 512)
                    P.tt(t1[:, cs], E.ps[hf][:], v0[:, cs], ALU.add)
                P.act(t1[:], t1[:], AF.Sigmoid)
                P.dma(vf[:], rows(S["VF"], ti))
                P.tt(vf[:], vf[:], v[:], ALU.subtract)
                P.tt(vf[:], vf[:], t1[:], ALU.mult)
                P.tt(v[:], v[:], vf[:], ALU.add)
            else:
                P.dma(rows(S["VF"], ti), v[:], q="sp")
            xmg = mix(5)
            for kc in range(8):
                P.mm(E.ps[2][:, 0:128], g1[:, kc, 0:128], xmg[:, kc, :], start=(kc == 0), stop=(kc == 7))
            for kc in range(8):
                P.mm(E.ps[3][0:32, 0:128], g1[:, kc, 128:160], xmg[:, kc, :], start=(kc == 0), stop=(kc == 7))
            ga = lo.next()
            gb = lo.next()
            P.act(ga[:], E.ps[2][:, 0:128], AF.Sigmoid)
            P.act(gb[0:32, :], E.ps[3][0:32, 0:128], AF.Sigmoid)
            for hf in range(2):
                cs = slice(hf * 512, (hf + 1) * 512)
                P.mm(E.ps[hf][:], ga[:], g2a[:, cs], start=True, stop=False)
                P.mm(E.ps[hf][:], gb[0:32, :], g2b[:, cs], start=False, stop=True)
            P.copy(g[:, 0:512], E.ps[0][:])
            P.copy(g[:, 512:1024], E.ps[1][:], eng="act")
            P.tt(kk[:], k[:], kkv[:, 0, :], ALU.mult)
            P.tt(t1[:], kk[:], kk[:], ALU.mult)
            P.red(sm[:, 0, :], t1[:].rr("p (h d) -> p h d", h=16))
            P.act(sm[:, 1, :], sm[:, 0, :], AF.Sqrt)
            P.ts(sm[:, 1, :], sm[:, 1, :], 1e-12, 0.0, ALU.max, ALU.add)
            P.recip(sm[:, 2, :], sm[:, 1, :])
            P.tt(kk[:].rr("p (h d) -> p h d", h=16), kk[:].rr("p (h d) -> p h d", h=16),
                 sm[:, 2, :].us(2).bc([128, 16, 64]), ALU.mult)
            P.ts(t1[:], kk[:], -1.0, 0.0, ALU.mult, ALU.add)
            for e in range(2):
                emit_q(e, t1, 0, ti)
                emit_q(e, v, 320, ti)
            xmw = mix(1)
            xma = mix(4)
            for e in range(2):
                lw = lora(xmw, lambda kc: w1[:, e, kc, :], 64, AF.Tanh)
                for hf in range(2):
                    cs = slice(hf * 512, (hf + 1) * 512)
                    P.mm(E.ps[hf][:], lw[0:64, :], w2[:, e, cs])
                    P.tt(we[:, cs], E.ps[hf][:], w0[:, e, cs], ALU.add)
                P.act(we[:], we[:], AF.Exp, scale=-1.0)
                P.act(we[:], we[:], AF.Ln, bias=1.0)
                P.act(we[:], we[:], AF.Exp, scale=-1.0, bias=-0.5)
                P.act(we[:], we[:], AF.Exp, scale=-1.0)
                la = lora(xma, lambda kc: a1[:, e, kc, :], 64, None)
                for hf in range(2):
                    cs = slice(hf * 512, (hf + 1) * 512)
                    P.mm(E.ps[hf][:], la[0:64, :], a2[:, e, cs])
                    P.tt(ae[:, cs], E.ps[hf][:], a0[:, e, cs], ALU.add)
                P.act(ae[:], ae[:], AF.Sigmoid)
                P.stt(kd[:], ae[:], -1.0, kkv[:, 1, :], ALU.add, ALU.mult)
                P.stt(kd[:], kd[:], 1.0, k[:], ALU.add, ALU.mult)
                if e == 0:
                    P.copy(kds[:], kd[:])
                else:
                    P.tt(kds[:], kds[:], kd[:], ALU.add)
                emit_q(e, we, 128, ti)
                emit_q(e, kd, 256, ti)
                P.tt(t1[:], we[:], r[:], ALU.mult)
                emit_q(e, t1, 64, ti)
                P.tt(t2[:], kk[:], ae[:], ALU.mult)
                emit_q(e, t2, 192, ti)
                P.tt(t2[:], t2[:], r[:], ALU.mult)
                P.red(sm[:, 3, :], t2[:].rr("p (h d) -> p h d", h=16))
                emit_s(e, sm[:, 3, :], 384, ti)
                P.tt(t2[:], kd[:], r[:], ALU.mult)
                P.red(sm[:, 4, :], t2[:].rr("p (h d) -> p h d", h=16))
                emit_s(e, sm[:, 4, :], 385, ti)
            P.tt(t1[:], r[:], kkv[:, 2, :], ALU.mult)
            P.tt(t1[:], t1[:], kds[:], ALU.mult)
            P.red(sm[:, 5, :], t1[:].rr("p (h d) -> p h d", h=16))
            P.dma(rows(S["RK"], ti), sm[:, 5, :], q="sp")
            P.dma(rows(S["VV"], ti), v[:], q="sp")
            P.dma(rows(S["GG"], ti), g[:], q="sp")


def st_scan(P, E, S, NT, TC=32):
    NTOK = NT * 128
    with Scope(P):
        St = Tile(P, "S", [128, 16, 64])
        Pt = Tile(P, "Pt", [128, 16, 64])
        inr = Ring(P, "scin", [128, TC, REC], 2)
        tmp = Tile(P, "stmp", [128, 16, 2, 64])
        tmp2 = Tile(P, "stmp2", [128, 16, 64])
        vkr = Ring(P, "vk", [128, 16, 64], 4, nb=16)
        say = Ring(P, "say", [128, TC, 16, 2], 2)
        vs = Ring(P, "vs", [128, TC, 16], 2)
        yr = Ring(P, "yy", [128, TC, 16], 2)
        y2 = Tile(P, "yy2", [128, TC, 16])
        P.memset(St[:], 0.0)
        for c in range(NTOK // TC):
            IN = inr.next()
            for e in range(2):
                base = S["SC"][e]
                src = bass.AP(base.tensor, base.offset + c * TC * REC, [[NTOK * REC, 16], [0, 4], [1, TC * REC]])
                P.dma(IN[e * 64:(e + 1) * 64].rr("p t r -> p (t r)"), DV(src))
            VS = vs.next()
            P.ts(VS[:], IN[:, :, 320:336], E.mask4[:, 0:1], 0.0, ALU.mult, ALU.add)
            for vb in range(1, 4):
                P.stt(VS[:], IN[:, :, 320 + vb * 16:336 + vb * 16], E.mask4[:, vb:vb + 1], VS[:],
                      ALU.mult, ALU.add)
            SAY = say.next()
            for t in range(TC):
                vk = vkr.next()
                for i in range(16):
                    P.act(vk.part(i)[:, i, :], IN[:, t, 256:320], AF.Copy, scale=VS[:, t, i:i + 1])
                P.tt(tmp[:], St[:].us(2).bc([128, 16, 2, 64]),
                     IN[:, t, 0:128].rr("p (a k) -> p a k", a=2).us(1).bc([128, 16, 2, 64]), ALU.mult)
                P.tt(Pt[:], St[:], IN[:, t, 128:192].us(1).bc([128, 16, 64]), ALU.mult, eng="pool")
                P.red(SAY[:, t], tmp[:])
                P.tt(Pt[:], Pt[:], vk[:], ALU.add, eng="pool")
                P.tt(tmp2[:], SAY[:, t, :, 0].us(2).bc([128, 16, 64]),
                     IN[:, t, 192:256].us(1).bc([128, 16, 64]), ALU.mult)
                P.tt(St[:], Pt[:], tmp2[:], ALU.add)
            Y = yr.next()
            P.tt(Y[:], SAY[:, :, :, 0], IN[:, :, 384:385].bc([128, TC, 16]), ALU.mult)
            P.tt(Y[:], Y[:], SAY[:, :, :, 1], ALU.add)
            P.tt(y2[:], VS[:], IN[:, :, 385:386].bc([128, TC, 16]), ALU.mult)
            P.tt(Y[:], Y[:], y2[:], ALU.add)
            for e in range(2):
                base = S["YO"][e]
                dst = bass.AP(base.tensor, base.offset + c * TC * 1024, [[16, 64], [1024, TC], [1, 16]])
                P.dma(DV(dst), Y[e * 64:(e + 1) * 64], q="sp")


def st_post_odd(P, E, X, W, j, S, NT, last):
    with Scope(P):
        xt = Tile(P, "xt", [128, 1024])
        y = Tile(P, "y", [128, 16, 64])
        yb = Tile(P, "yb", [128, 1024])
        v = Tile(P, "v", [128, 16, 64])
        g = Tile(P, "g", [128, 1024])
        rk = Tile(P, "rk", [128, 16])
        lnw = Tile(P, "lnw", [128, 2, 1024])
        sm = Tile(P, "sm16", [128, 4, 16])
        oT = Tile(P, "oT", [128, 8, 128], dt=F32R)
        wkr = Ring(P, "wk", [128, 1024], 4, dt=F32R)
        tmp = Tile(P, "tmpy", [128, 512])
        P.dma(lnw[:, 0], DV(W["tm_ln_w"][j].partition_broadcast(128)))
        P.dma(lnw[:, 1], DV(W["tm_ln_b"][j].partition_broadcast(128)))
        for ti in range(NT):
            if last and ti < 2:
                continue
            P.dma(xt[:], rows(X, ti))
            P.dma(y[:].rr("p h d -> p (h d)"), rows(S["YO"][0], ti))
            so = rev_step_off(ti, NT)
            P.dma(yb[:], DV(S["YO"][1][so:so + 128]))
            P.dma(v[:].rr("p h d -> p (h d)"), rows(S["VV"], ti))
            P.dma(g[:], rows(S["GG"], ti))
            P.dma(rk[:], rows(S["RK"], ti))
            for hf in range(2):
                P.mm(E.ps[hf][:], E.jrev[:], yb[:, hf * 512:(hf + 1) * 512])
                P.tt(y[:, hf * 8:(hf + 1) * 8, :], y[:, hf * 8:(hf + 1) * 8, :],
                     E.ps[hf][:].rr("p (h d) -> p h d", h=8), ALU.add)
            P.red(sm[:, 0, :], y[:])
            P.ts(sm[:, 0, :], sm[:, 0, :], 1.0 / 64, 0.0, ALU.mult, ALU.add)
            P.tt(y[:], y[:], sm[:, 0, :].us(2).bc([128, 16, 64]), ALU.subtract)
            yq = yb[:].rr("p (h d) -> p h d", h=16)
            P.tt(yq, y[:], y[:], ALU.mult)
            P.red(sm[:, 1, :], yq)
            P.act(sm[:, 2, :], sm[:, 1, :], AF.Sqrt, bias=64e-5, scale=1.0 / 64)
            P.recip(sm[:, 3, :], sm[:, 2, :])
            P.tt(y[:], y[:], sm[:, 3, :].us(2).bc([128, 16, 64]), ALU.mult)
            yf = y[:].rr("p h d -> p (h d)")
            P.tt(yf, yf, lnw[:, 0, :], ALU.mult)
            P.tt(yf, yf, lnw[:, 1, :], ALU.add)
            P.tt(v[:], v[:], rk[:].us(2).bc([128, 16, 64]), ALU.mult)
            P.tt(y[:], y[:], v[:], ALU.add)
            P.tt(yf, yf, g[:], ALU.mult)
            emit_transpose(P, E, oT[:], yf, 8)
            for kc in range(8):
                wt = wkr.next()
                P.dma(wt[:], DV(W["tm_w_o"][j, kc * 128:(kc + 1) * 128, :]), q="pool")
                for hf in range(2):
                    P.mm(E.ps[hf][:], oT[:, kc, :], wt[:, hf * 512:(hf + 1) * 512],
                         start=(kc == 0), stop=(kc == 7), inc=(hf == 1))
            emit_postadd(P, E, tmp, xt[:], [E.ps[0][:], E.ps[1][:]], typ_of(ti))
            P.dma(rows(X, ti), xt[:], q="sp")


def tile_w_in(w):
    g = w[:, :FH].reshape(8, 128, 22, 128).transpose(2, 1, 0, 3)
    u = w[:, FH:].reshape(8, 128, 22, 128).transpose(2, 1, 0, 3)
    return np.ascontiguousarray(np.concatenate([g, u], axis=-1))


def tile_ev_in(w):
    wp = np.zeros((D, 45 * 128), np.float32)
    wp[:, :w.shape[1]] = w
    return np.ascontiguousarray(wp.reshape(8, 128, 45, 128).transpose(2, 1, 0, 3))


def rope_tables(L):
    t = np.arange(L)
    row = (t // 64).astype(np.float32)
    col = (t % 64).astype(np.float32)
    inv = (np.float32(10000.0) ** (-np.arange(16, dtype=np.float32) / np.float32(16))).astype(np.float32)
    ang = np.stack([row[:, None] * inv, col[:, None] * inv])
    f = np.arange(128)
    a = (f % 64) // 32
    i = f % 16
    A = ang[a, :, i]
    return np.cos(A).astype(np.float32), np.sin(A).astype(np.float32)


def make_consts():
    ident = np.eye(128, dtype=np.float32)
    jrev = np.ascontiguousarray(ident[::-1])
    rot = np.zeros((128, 128), np.float32)
    for fp in range(128):
        if (fp % 32) < 16:
            rot[fp + 16, fp] = -1.0
        else:
            rot[fp - 16, fp] = 1.0
    jj, ii = np.meshgrid(np.arange(128), np.arange(128), indexing="ij")
    tri0 = (jj <= ii).astype(np.float32)
    tri1 = (jj >= ii).astype(np.float32)
    mask4 = np.zeros((128, 4), np.float32)
    mask4[np.arange(128), np.arange(128) % 4] = 1.0
    return dict(ident=ident, jrev=jrev, rot=rot, tri0=tri0, tri1=tri1, mask4=mask4)


DEPTH = 4


def build_and_run(inputs, ncores=4, depth=DEPTH, debug_out=None):
    x = np.asarray(inputs["x"], np.float32)
    B, L, _ = x.shape
    NTOK = CTX + L
    NT = NTOK // 128
    LA = Launch(ncores)
    nc = LA.nc
    f = lambda a: np.asarray(a, np.float32)
    ctx = f(inputs["ctx"])
    c = f(inputs["c"])
    c_ctx = f(inputs["c_ctx"])
    XIN = LA.inp("xin", [np.concatenate([ctx[b % B], x[b % B]], axis=0) for b in range(ncores)])
    C = {k: LA.inp(k, v) for k, v in make_consts().items()}
    cv = []
    for b in range(ncores):
        vv = np.stack([c[b % B], c_ctx], axis=-1)
        cv.append(np.ascontiguousarray(vv.reshape(8, 128, 2).transpose(1, 0, 2)))
    C["cvec"] = LA.inp("cvec", cv)
    W = {}
    cosT, sinT = rope_tables(L)
    W["cosT"] = LA.inp("cosT", cosT)
    W["sinT"] = LA.inp("sinT", sinT)
    for name in ("ada_w", "ada_b", "norm_w", "ffn_w_out", "ev_w_out", "da_lambda", "da_subln", "ssd_d",
                 "ssd_norm", "tm_mu", "tm_w_r", "tm_w_k", "tm_w_v", "tm_w_o", "tm_w0", "tm_w1", "tm_w2",
                 "tm_a0", "tm_a1", "tm_a2", "tm_g1", "tm_g2", "tm_k_k", "tm_k_a", "tm_r_k", "tm_ln_w",
                 "tm_ln_b", "tm_v0", "tm_v1", "tm_v2"):
        W[name] = LA.inp(name, f(inputs[name]))
    wi = f(inputs["ffn_w_in"])
    W["ffn_w_in"] = LA.inp("ffn_w_in", np.stack([np.stack([tile_w_in(wi[l, s]) for s in range(2)])
                                                 for l in range(wi.shape[0])]))
    ev = f(inputs["ev_w_in"])
    W["ev_w_in"] = LA.inp("ev_w_in", np.stack([tile_ev_in(ev[j]) for j in range(ev.shape[0])]))
    cw = f(inputs["ssd_conv_w"])
    cb = f(inputs["ssd_conv_b"])
    conv = np.concatenate([cw, cb[:, None, :]], axis=1)
    W["ssd_conv"] = LA.inp("ssd_conv", np.ascontiguousarray(conv.reshape(-1, 6, 12, 128).transpose(0, 3, 2, 1)))
    W["ssd_dt_bias"] = LA.inp("ssd_dt_bias", f(inputs["ssd_dt_bias"]).reshape(-1, 32))
    W["ssd_a_log"] = LA.inp("ssd_a_log", f(inputs["ssd_a_log"]).reshape(-1, 32))
    OUT = LA.out("out", [L, D])
    S = {}
    X = LA.scratch("X", [NTOK, D])
    for nm, shp in (("QT", [8, 128, NTOK]), ("KT", [8, 128, NTOK]), ("V", [NTOK, D]), ("Z", [NTOK, D]),
                    ("XBCT", [12, 128, NTOK]), ("DT", [NTOK, 32]), ("MIXT", [8, 128, NTOK]),
                    ("BCT", [4, 128, NTOK]), ("XS", [NTOK, D]), ("BTOK", [NTOK, 256]), ("YS", [NTOK, D]),
                    ("HT", [8, 128, NTOK]), ("VF", [NTOK, D]), ("RK", [NTOK, 16]), ("VV", [NTOK, D]),
                    ("GG", [NTOK, D])):
        S[nm] = LA.scratch(nm, shp) if (debug_out is None or nm not in debug_out) else LA.out(nm, shp)
    S["SC"] = [LA.scratch(f"SC{e}", [16, NTOK, REC]) for e in range(2)]
    if debug_out is not None and "YO" in debug_out:
        S["YO"] = [LA.out(f"YO{e}", [NTOK, D]) for e in range(2)]
    else:
        S["YO"] = [LA.scratch(f"YO{e}", [NTOK, D]) for e in range(2)]
    if debug_out is not None and "X" in debug_out:
        XD = LA.out("XD", [NTOK, D])
    with ExitStack() as es:
        P = Prog(nc, es)
        E = Env(P, C)
        for l in range(depth):
            j = l // 2
            last = l == DEPTH - 1
            emit_mods(P, E, W, l, 0, 0.5)
            st_ffn(P, E, XIN if l == 0 else X, X, W["ffn_w_in"][l, 0], W["ffn_w_out"][l, 0], NT)
            emit_mods(P, E, W, l, 1, 1.0)
            if l % 2 == 0:
                st_pre_even(P, E, X, W, j, S, NT)
                st_attn(P, E, W, j, l, S, NT)
                st_ssd(P, E, W, j, S, NT)
                st_post_even(P, E, X, W, j, S, NT)
            else:
                st_pre_odd(P, E, X, S, NT)
                st_prep_odd(P, E, W, j, S, NT, use_vgate=(j > 0))
                st_scan(P, E, S, NT)
                st_post_odd(P, E, X, W, j, S, NT, last)
            emit_mods(P, E, W, l, 2, 0.5)
            st_ffn(P, E, X, X, W["ffn_w_in"][l, 1], W["ffn_w_out"][l, 1], NT,
                   out_lat=(OUT if l == depth - 1 else None))
        if debug_out is not None and "X" in debug_out:
            with Scope(P):
                t = Tile(P, "dbg", [128, 1024])
                for ti in range(NT):
                    P.dma(t[:], rows(X, ti))
                    P.dma(rows(XD, ti), t[:])
        P.barrier()
    print("instructions:", P.nins, flush=True)
    res = LA.run()
    return res


def kernel(**inputs):
    x = np.asarray(inputs["x"])
    B = x.shape[0]
    res = build_and_run(inputs, ncores=B)
    return np.stack([np.asarray(res[b]["out"], np.float32) for b in range(B)], axis=0)
```
